# Optimizing a Trainium2 kernel written in Bass

```python
import math
import jax, jax.numpy as jnp
from jax import lax
import numpy as np

D_MODEL = 2048
BATCH = 2
SEQ = 16384
DEPTH = 1
DEC_BATCH = 16
DEC_SEQ = 64
PAST_LEN = 2048

CHUNK = 64
MLP_WIDTH = D_MODEL // 2
ATTN_WIDTH = D_MODEL - MLP_WIDTH
MLP_GROUPS = 8
MLP_GROUP_DIM = MLP_WIDTH // MLP_GROUPS
MLP_CHUNK = 128
N_HEADS = 8
HEAD_DIM = ATTN_WIDTH // (2 * N_HEADS)
V_HEAD_DIM = 2 * HEAD_DIM
QK_WIDTH = N_HEADS * 2 * HEAD_DIM
V_WIDTH = N_HEADS * V_HEAD_DIM
IN_WIDTH = 2 * MLP_WIDTH + 2 * QK_WIDTH + V_WIDTH
D_FF = ((8 * D_MODEL // 3 + 255) // 256) * 256
N_BUCKETS = 32
MAX_DISTANCE = 128
Q_BLOCK = 128
EPS = 1e-6

kernel_name = 'hybrid_gmlp_diffattn_stream_step'


def rmsnorm(x, g):
    xf = x.astype(jnp.float32)
    y = xf * lax.rsqrt(jnp.mean(xf * xf, axis=-1, keepdims=True) + EPS)
    return (y * g.astype(jnp.float32)).astype(x.dtype)


def layernorm(x, g, b):
    xf = x.astype(jnp.float32)
    mu = jnp.mean(xf, axis=-1, keepdims=True)
    var = jnp.mean(jnp.square(xf - mu), axis=-1, keepdims=True)
    y = (xf - mu) * lax.rsqrt(var + EPS)
    return (y * g.astype(jnp.float32) + b.astype(jnp.float32)).astype(x.dtype)


def modulate(h, shift, scale):
    return h * (1 + scale[:, None, :]) + shift[:, None, :]


def adaln(c, w, b, n):
    m = jnp.einsum('bd,de->be', jax.nn.silu(c), w) + b
    return jnp.split(m, n, axis=-1)


def t5_bucket(rel):
    nb = N_BUCKETS // 2
    max_exact = nb // 2
    ret = jnp.where(rel > 0, nb, 0)
    n = jnp.abs(rel)
    nf = jnp.maximum(n, 1).astype(jnp.float32)
    large = max_exact + (jnp.log(nf / max_exact) / math.log(MAX_DISTANCE / max_exact)
                         * (nb - max_exact)).astype(jnp.int32)
    large = jnp.minimum(large, nb - 1)
    return ret + jnp.where(n < max_exact, n, large)


def diff_attention(q, k, v, q_pos, k_pos, lam, rel_bias):
    bias = jnp.transpose(rel_bias[t5_bucket(k_pos[None, :] - q_pos[:, None])], (2, 0, 1)).astype(jnp.float32)
    allowed = (k_pos[None, :] // CHUNK) <= (q_pos[:, None] // CHUNK)
    logits = jnp.einsum('bqhtd,bkhtd->bthqk', q, k).astype(jnp.float32) * (HEAD_DIM ** -0.5) + bias[None, None]
    logits = jnp.where(allowed[None, None, None], logits, -jnp.inf)
    probs = jax.nn.softmax(logits, axis=-1)
    attn = probs[:, 0] - lam * probs[:, 1]
    return jnp.einsum('bhqk,bkhd->bqhd', attn.astype(v.dtype), v)


def prompt_attention(q, k, v, pos, lam, rel_bias):
    B, S = q.shape[:2]
    nb = S // Q_BLOCK
    qb = jnp.swapaxes(q.reshape(B, nb, Q_BLOCK, N_HEADS, 2, HEAD_DIM), 0, 1)
    pb = pos.reshape(nb, Q_BLOCK)
    out = lax.map(lambda a: diff_attention(a[0], k, v, a[1], pos, lam, rel_bias), (qb, pb))
    return jnp.swapaxes(out, 0, 1).reshape(B, S, N_HEADS, V_HEAD_DIM)


def diff_lambda(lq1, lk1, lq2, lk2, lam_init):
    f = jnp.float32
    return (jnp.exp(jnp.sum(lq1.astype(f) * lk1.astype(f))) -
            jnp.exp(jnp.sum(lq2.astype(f) * lk2.astype(f))) + lam_init)


def mixer_in(x, shift, scale, g_norm, w_in, ln_g, ln_b):
    B, S = x.shape[:2]
    h = modulate(rmsnorm(x, g_norm), shift, scale)
    z = jnp.einsum('bsd,de->bse', h, w_in)
    o1 = MLP_WIDTH
    o2 = 2 * MLP_WIDTH
    o3 = o2 + QK_WIDTH
    o4 = o3 + QK_WIDTH
    u = jax.nn.gelu(z[..., :o1])
    gv = layernorm(jax.nn.gelu(z[..., o1:o2]), ln_g, ln_b)
    q = z[..., o2:o3].reshape(B, S, N_HEADS, 2, HEAD_DIM)
    k = z[..., o3:o4].reshape(B, S, N_HEADS, 2, HEAD_DIM)
    v = z[..., o4:].reshape(B, S, N_HEADS, V_HEAD_DIM)
    return u, gv, q, k, v


def spatial_gate(u, gv, w_s, b_s):
    T = u.shape[2]
    idx = jnp.arange(T)
    mask = (idx[None, :] // CHUNK) <= (idx[:, None] // CHUNK)
    w = jnp.where(mask[None], w_s[:, :T, :T], 0)
    mixed = jnp.einsum('gij,bnjgc->bnigc', w, gv) + jnp.transpose(b_s[:, :T])[None, None, :, :, None]
    return u * mixed


def mixer_out(x, m, a, gate, sub_g, lam_init, w_out):
    B, S = x.shape[:2]
    a = rmsnorm(a, sub_g) * (1 - lam_init)
    cat = jnp.concatenate([m.reshape(B, S, MLP_WIDTH), a.reshape(B, S, ATTN_WIDTH)], axis=-1)
    return x + gate[:, None, :] * jnp.einsum('bse,ed->bsd', cat, w_out)


def ffn_sublayer(x, shift, scale, gate, g_norm, w_ffn_in, w_ffn_out):
    h = modulate(rmsnorm(x, g_norm), shift, scale)
    zg, zu = jnp.split(jnp.einsum('bsd,df->bsf', h, w_ffn_in), 2, axis=-1)
    return x + gate[:, None, :] * jnp.einsum('bsf,fd->bsd', jax.nn.silu(zg) * zu, w_ffn_out)


def setup_inputs(seed: int = 0) -> dict:
    key = jax.random.key(seed)
    ks = jax.random.split(key, 32)
    f = jnp.float32
    nrm = lambda k, s, sc: jax.random.normal(k, s, f) * sc
    D = D_MODEL
    return {
        'x_prompt': nrm(ks[0], (BATCH, SEQ, D), 1.0),
        'x_sample': nrm(ks[1], (DEC_BATCH, DEC_SEQ, D), 1.0),
        'cache_k': nrm(ks[2], (DEPTH, DEC_BATCH, PAST_LEN, N_HEADS, 2 * HEAD_DIM), 1.0),
        'cache_v': nrm(ks[3], (DEPTH, DEC_BATCH, PAST_LEN, N_HEADS, V_HEAD_DIM), 1.0),
        'c_prompt': nrm(ks[4], (BATCH, D), 1.0),
        'c_sample': nrm(ks[5], (DEC_BATCH, D), 1.0),
        'rel_bias': nrm(ks[6], (N_BUCKETS, N_HEADS), 0.5),
        'w_ada': nrm(ks[7], (DEPTH, D, 6 * D), 0.5 * D ** -0.5),
        'b_ada': nrm(ks[8], (DEPTH, 6 * D), 0.01),
        'w_ada_final': nrm(ks[9], (D, 2 * D), 0.5 * D ** -0.5),
        'b_ada_final': nrm(ks[10], (2 * D,), 0.01),
        'g_mix': 1.0 + nrm(ks[11], (DEPTH, D), 0.01),
        'g_ffn': 1.0 + nrm(ks[12], (DEPTH, D), 0.01),
        'g_final': 1.0 + nrm(ks[13], (D,), 0.01),
        'w_in': nrm(ks[14], (DEPTH, D, IN_WIDTH), D ** -0.5),
        'mlp_ln_g': 1.0 + nrm(ks[15], (DEPTH, MLP_WIDTH), 0.01),
        'mlp_ln_b': nrm(ks[16], (DEPTH, MLP_WIDTH), 0.01),
        'w_s': nrm(ks[17], (DEPTH, MLP_GROUPS, MLP_CHUNK, MLP_CHUNK), MLP_CHUNK ** -0.5),
        'b_s': 1.0 + nrm(ks[18], (DEPTH, MLP_GROUPS, MLP_CHUNK), 0.01),
        'lambda_q1': nrm(ks[19], (DEPTH, HEAD_DIM), 0.1),
        'lambda_k1': nrm(ks[20], (DEPTH, HEAD_DIM), 0.1),
        'lambda_q2': nrm(ks[21], (DEPTH, HEAD_DIM), 0.1),
        'lambda_k2': nrm(ks[22], (DEPTH, HEAD_DIM), 0.1),
        'sub_g': 1.0 + nrm(ks[23], (DEPTH, V_HEAD_DIM), 0.01),
        'w_out': nrm(ks[24], (DEPTH, D, D), D ** -0.5),
        'w_ffn_in': nrm(ks[25], (DEPTH, D, 2 * D_FF), D ** -0.5),
        'w_ffn_out': nrm(ks[26], (DEPTH, D_FF, D), D_FF ** -0.5),
    }


def reference(x_prompt, x_sample, cache_k, cache_v, c_prompt, c_sample, rel_bias,
              w_ada, b_ada, w_ada_final, b_ada_final, g_mix, g_ffn, g_final,
              w_in, mlp_ln_g, mlp_ln_b, w_s, b_s, lambda_q1, lambda_k1, lambda_q2, lambda_k2,
              sub_g, w_out, w_ffn_in, w_ffn_out):
    B, S, _ = x_prompt.shape
    DB, T, _ = x_sample.shape
    nc = S // MLP_CHUNK
    pos_p = jnp.arange(S, dtype=jnp.int32)
    q_pos_s = PAST_LEN + jnp.arange(T, dtype=jnp.int32)
    k_pos_s = jnp.arange(PAST_LEN + T, dtype=jnp.int32)
    xp, xs = x_prompt, x_sample
    kp_rows, vp_rows, ks_rows, vs_rows, gvs_rows = [], [], [], [], []
    for l in range(DEPTH):
        lam_init = 0.8 - 0.6 * math.exp(-0.3 * l)
        lam = diff_lambda(lambda_q1[l], lambda_k1[l], lambda_q2[l], lambda_k2[l], lam_init)

        sh1, sc1, gt1, sh2, sc2, gt2 = adaln(c_prompt, w_ada[l], b_ada[l], 6)
        u, gv, q, k, v = mixer_in(xp, sh1, sc1, g_mix[l], w_in[l], mlp_ln_g[l], mlp_ln_b[l])
        m = spatial_gate(u.reshape(B, nc, MLP_CHUNK, MLP_GROUPS, MLP_GROUP_DIM),
                         gv.reshape(B, nc, MLP_CHUNK, MLP_GROUPS, MLP_GROUP_DIM), w_s[l], b_s[l])
        a = prompt_attention(q, k, v, pos_p, lam, rel_bias)
        xp = mixer_out(xp, m, a, gt1, sub_g[l], lam_init, w_out[l])
        xp = ffn_sublayer(xp, sh2, sc2, gt2, g_ffn[l], w_ffn_in[l], w_ffn_out[l])
        kp_rows.append(k.reshape(B, S, N_HEADS, 2 * HEAD_DIM))
        vp_rows.append(v)

        sh1, sc1, gt1, sh2, sc2, gt2 = adaln(c_sample, w_ada[l], b_ada[l], 6)
        u, gv, q, k, v = mixer_in(xs, sh1, sc1, g_mix[l], w_in[l], mlp_ln_g[l], mlp_ln_b[l])
        gv_s = gv.reshape(DB, T, MLP_GROUPS, MLP_GROUP_DIM)
        m = spatial_gate(u.reshape(DB, 1, T, MLP_GROUPS, MLP_GROUP_DIM), gv_s[:, None], w_s[l], b_s[l])
        k_all = jnp.concatenate([cache_k[l].reshape(DB, PAST_LEN, N_HEADS, 2, HEAD_DIM), k], axis=1)
        v_all = jnp.concatenate([cache_v[l], v], axis=1)
        a = diff_attention(q, k_all, v_all, q_pos_s, k_pos_s, lam, rel_bias)
        xs = mixer_out(xs, m, a, gt1, sub_g[l], lam_init, w_out[l])
        xs = ffn_sublayer(xs, sh2, sc2, gt2, g_ffn[l], w_ffn_in[l], w_ffn_out[l])
        ks_rows.append(k.reshape(DB, T, N_HEADS, 2 * HEAD_DIM))
        vs_rows.append(v)
        gvs_rows.append(gv_s)

    shp, scp = adaln(c_prompt, w_ada_final, b_ada_final, 2)
    shs, scs = adaln(c_sample, w_ada_final, b_ada_final, 2)
    y_prompt = modulate(rmsnorm(xp, g_final), shp, scp)
    y_sample = modulate(rmsnorm(xs, g_final), shs, scs)
    new_k_prompt = jnp.stack(kp_rows)
    new_v_prompt = jnp.stack(vp_rows)
    new_k_sample = jnp.stack(ks_rows)
    new_v_sample = jnp.stack(vs_rows)
    new_gv_sample = jnp.stack(gvs_rows)
    return (y_prompt, y_sample, new_k_prompt, new_v_prompt, new_k_sample, new_v_sample, new_gv_sample)
```

```python
import math
from contextlib import ExitStack

import numpy as np
import concourse.bass as bass
import concourse.mybir as mybir
from concourse.bass_utils import run_bass_kernel_spmd

F32 = mybir.dt.float32
BF16 = mybir.dt.bfloat16
AF = mybir.ActivationFunctionType
ALU = mybir.AluOpType
AX = mybir.AxisListType

D = 2048
DC = 16
NH = 8
DFF = 5632
FC = 44
EPS = 1e-6
PAST = 2048
NEG = -30000.0
LAM_INIT = 0.8 - 0.6 * math.exp(-0.3 * 0)
NG = 1152
SAME_ENG_SYNC = True
import os as _os
STOP = int(_os.environ.get('KSTOP', '99'))
KSUB = int(_os.environ.get('KSUB', '99'))
KCORES = int(_os.environ.get('KCORES', '8'))
KVAR = _os.environ.get('KVAR', 'ABCD')


class Sem:
    def __init__(self, h):
        self.h = h
        self.total = 0


class Op:
    __slots__ = ("eng", "fn", "deps", "dma_sem", "token", "need", "is_dma", "ninst")

    def __init__(self, eng, fn, dma_sem=None, ninst=1):
        self.eng = eng
        self.fn = fn
        self.deps = []
        self.dma_sem = dma_sem
        self.is_dma = dma_sem is not None
        self.token = None
        self.need = False
        self.ninst = ninst


class Prog:
    ENGS = ("sp", "pool", "pe", "act", "dve")

    def __init__(self, nc, es):
        self.nc = nc
        self.es = es
        self.ops = {e: [] for e in self.ENGS}
        self.res = {}
        self.eng_sem = {}
        for e in ("pool", "pe", "act", "dve"):
            self.eng_sem[e] = Sem(es.enter_context(nc.semaphore("es_" + e)))
        self.nsem = 0
        self.out_ops = []

    def new_sem(self):
        self.nsem += 1
        return Sem(self.es.enter_context(self.nc.semaphore("ds%d" % self.nsem)))

    RENAME = {"ps0": "pss0", "ps1": "pss0", "ps2": "pss1", "ps3": "pss1", "accb0": "gvf0", "accb1": "gvf0"}

    def op(self, eng, fn, reads=(), writes=(), dma_sem=None, ninst=1):
        reads = [self.RENAME.get(r, r) for r in reads]
        writes = [self.RENAME.get(w, w) for w in writes]
        o = Op(eng, fn, dma_sem, ninst)
        deps = {}
        for r in reads:
            st = self.res.get(r)
            if st is not None and st[0] is not None:
                deps[id(st[0])] = (st[0], True)
        for w in writes:
            st = self.res.get(w)
            if st is not None:
                if st[0] is not None and id(st[0]) not in deps:
                    deps[id(st[0])] = (st[0], False)
                for rd in st[1]:
                    if id(rd) not in deps:
                        deps[id(rd)] = (rd, False)
        for r in reads:
            st = self.res.setdefault(r, [None, []])
            if not o.is_dma:
                st[1] = [x for x in st[1] if x.is_dma or x.eng != eng]
            st[1].append(o)
        for w in writes:
            self.res[w] = [o, []]
        o.deps = [v for v in deps.values() if v[0] is not o]
        if o.is_dma:
            prev = dma_sem.total
            dma_sem.total += 16 * ninst
            o.token = (dma_sem, dma_sem.total)
            o.deps.append((("SEM", dma_sem, prev), True))
        self.ops[eng].append(o)
        return o

    def add_writes(self, o, names):
        for w in names:
            self.res[w] = [o, []]

    def finalize(self):
        for e in self.ENGS:
            for o in self.ops[e]:
                for d, raw in o.deps:
                    if isinstance(d, tuple):
                        continue
                    if d.is_dma:
                        continue
                    if d.eng == o.eng:
                        if SAME_ENG_SYNC and d.eng in ("act", "dve", "pool"):
                            d.need = True
                        continue
                    d.need = True
        for e in ("pool", "pe", "act", "dve"):
            s = self.eng_sem[e]
            for o in self.ops[e]:
                if not o.is_dma and o.need:
                    s.total += 1
                    o.token = (s, s.total)

    def emit(self, eng_name, eng):
        waited = {}
        for o in self.ops[eng_name]:
            need = {}
            for d, raw in o.deps:
                if isinstance(d, tuple):
                    _, s, v = d
                    if v <= 0:
                        continue
                else:
                    if (not d.is_dma) and d.eng == o.eng:
                        if not (SAME_ENG_SYNC and d.eng in ("act", "dve", "pool")):
                            continue
                    s, v = d.token
                if need.get(id(s), (s, 0))[1] < v:
                    need[id(s)] = (s, v)
            for s, v in need.values():
                if waited.get(id(s), 0) >= v:
                    continue
                eng.wait_ge(s.h, v)
                waited[id(s)] = v
            r = o.fn(eng)
            if o.is_dma:
                if not isinstance(r, (list, tuple)):
                    r = [r]
                assert len(r) == o.ninst
                for ins in r:
                    ins.then_inc(o.dma_sem.h, 16)
            elif o.need:
                r.then_inc(o.token[0].h, 1)


def build_nc(NT):
    NSLOT = NT // 4
    NTP = NT * 512
    NOWN = NSLOT * 512
    nc = bass.Bass("TRN2", target_bir_lowering=False)

    def din(name, shape, dt=F32):
        return nc.dram_tensor(name, list(shape), dt, kind="ExternalInput").ap()

    def dout(name, shape, dt=F32):
        return nc.dram_tensor(name, list(shape), dt, kind="ExternalOutput").ap()

    xsT = din("xsT", [D, NTP])
    xsmpT = din("xsmpT", [D, 128])
    ckT = din("ckT", [2, NH, 128, PAST])
    cvB = din("cvB", [2, NH, 128, 16, 128])
    cT = din("cT", [128, DC, 3])
    padflag = din("padflag", [128, 4])
    w_ada = din("w_ada", [D, 6 * D])
    b_adaT = din("b_adaT", [128, 96])
    w_adaf = din("w_adaf", [D, 2 * D])
    b_adafT = din("b_adafT", [128, 32])
    g3T = din("g3T", [128, 3, DC])
    w_in = din("w_in", [D, 5120])
    lnGB = din("lnGB", [128, 2, 1024])
    w_sT = din("w_sT", [128, 8, 128])
    bsB = din("bsB", [128, 8, 128])
    lamv = din("lamv", [128, 256])
    subgT = din("subgT", [128, 1])
    relb = din("relb", [32, 8])
    ohp = din("ohp", [32, NG])
    cmask = din("cmask", [128, 1024])
    w_out = din("w_out", [D, D])
    w_ffn_in = din("w_ffn_in", [D, 2 * DFF])
    w_ffn_out = din("w_ffn_out", [DFF, D])

    yT = dout("yT", [D, NOWN])
    ysT = dout("ysT", [D, 128])
    kT_out = dout("kT_out", [NH, 128, NOWN])
    v_out = dout("v_out", [NOWN, 1024])
    ksT_out = dout("ksT_out", [NH, 128, 128])
    vs_out = dout("vs_out", [128, 1024])
    gvs_out = dout("gvs_out", [128, 1024])

    KT = nc.dram_tensor("KTscr", [NH, 128, NTP], BF16, kind="Internal").ap()
    VB = nc.dram_tensor("VBscr", [NH, 128, NT * 4, 128], BF16, kind="Internal").ap()
    Gd2 = nc.dram_tensor("Gd2scr", [8, NG], F32, kind="Internal").ap()
    WS_in = nc.dram_tensor("WSin", [10, 128, DC, 512], BF16, kind="Internal").ap()
    WS_out = nc.dram_tensor("WSout", [4, 128, DC, 512], BF16, kind="Internal").ap()
    WS_fi = nc.dram_tensor("WSfi", [22, 128, DC, 512], BF16, kind="Internal").ap()
    WS_fo = nc.dram_tensor("WSfo", [DC, 128, FC, 128], BF16, kind="Internal").ap()

    es = ExitStack()
    with es:
        def sb(name, shape, dt=F32):
            return es.enter_context(nc.sbuf_tensor(name, list(shape), dt))

        P = Prog(nc, es)

        XH = [sb("xh%d" % i, [128, DC, 256]) for i in range(2)]
        hT = sb("hT", [128, DC, 512], BF16)
        WB = [sb("wb%d" % i, [128, DC, 512], BF16) for i in range(2)]
        BIG = sb("big", [128, 29696], BF16)
        sq = sb("sq", [128, DC, 256], BF16)
        Bf = sb("Bf", [128, NH, 1024], BF16)
        lnGB_sb = sb("lnGB_sb", [128, 2, 1024])
        bsB_sb = sb("bsB_sb", [128, 8, 128])
        wsT_sb = sb("wsT_sb", [128, 8, 128], BF16)
        gvf = sb("gvf", [128, 1, 1024])
        gvb = sb("gvb", [128, 1, 1024], BF16)
        M1 = sb("M1", [128, 96, 3])
        MF = sb("MF", [128, 32, 3])
        TAB = sb("TAB", [128, 3, 8, DC])
        cT_sb = sb("cT_sb", [128, DC * 3])
        siluT = sb("siluT", [128, DC, 3], BF16)
        badaT_sb = sb("badaT_sb", [128, 96])
        badafT_sb = sb("badafT_sb", [128, 32])
        g3T_sb = sb("g3T_sb", [128, 3, DC])
        lam_sb = sb("lam_sb", [128, 256])
        lamt = sb("lamt", [128, 8])
        subg_sb = sb("subg_sb", [128, 2])
        pad_sb = sb("pad_sb", [128, 4])
        relb_sb = sb("relb_sb", [32, 8])
        ones_bf = sb("ones_bf", [128, 128], BF16)
        rstd = sb("rstd", [128, 512])
        TMP = [sb("tmp%d" % i, [128, 512]) for i in range(4)]
        STF = [sb("stf%d" % i, [128, 512]) for i in range(2)]
        STB = [sb("stb%d" % i, [128, 1024], BF16) for i in range(2)]
        small = sb("small", [128, 16])

        def big(off, n, shape3=None):
            ap = BIG[:, off:off + n]
            if shape3 is not None:
                ap = ap.rearrange("p (a b) -> p a b", a=shape3[0], b=shape3[1])
            return ap
        Wk = big(0, 16384, (DC, 1024))
        QT = big(0, 4096, (NH, 512))
        catT = big(4096, 8192, (DC, 512))
        KBUF = [big(12288 + i * 2048, 2048) for i in range(2)]
        VBUF = [big(16384 + i * 2048, 2048, (16, 128)) for i in range(2)]
        PT = [[big(20480 + (i * 2 + t) * 512, 512) for t in range(2)] for i in range(2)]
        actT = big(0, 22528, (FC, 512))
        uT = big(22528, 4096, (NH, 512))
        ksT = big(26624, 1024, (NH, 128))
        vsb = big(27648, 2048, (2, 1024))
        xf0 = XH[0].rearrange("p a b -> p (a b)")
        xf1 = XH[1].rearrange("p a b -> p (a b)")
        cmask_f = xf0[:, 0:1024]
        ohp_sb = xf0[:, 1024:1024 + NG]
        g_sb = xf0[:, 1024 + NG:1024 + 2 * NG]
        bfr_f = [xf1[:, i * 1024:(i + 1) * 1024] for i in range(2)]
        XALL = ["xh%d_%d" % (a, b) for a in range(2) for b in range(DC)]

        PSS = [es.enter_context(nc.psum_tensor("pss%d" % i, [128, 1024], F32)) for i in range(2)]
        PS = [PSS[0][:, 0:512], PSS[0][:, 512:1024], PSS[1][:, 0:512], PSS[1][:, 512:1024]]
        PS += [es.enter_context(nc.psum_tensor("ps%d" % i, [128, 512], F32)) for i in range(4, 8)]
        PTP = [big(20480 + i * 1024, 1024) for i in range(2)]
        ACC = sq.rearrange("p a b -> p (a b)").bitcast(F32).rearrange("p (a b) -> p a b", a=4, b=512)
        ACCRES = [["sq%d" % (4 * i + q) for q in range(4)] for i in range(4)]
        ACCB = gvf.rearrange("p a b -> p (a b)").bitcast(BF16).rearrange("p (a b) -> p a b", a=4, b=512)

        def dma(q, out, in_, reads, writes, sem):
            return P.op(q, lambda e: e.dma_start(out=out, in_=in_), reads, writes, dma_sem=sem)

        def mm(out, lhsT, rhs, start, stop, reads, writes, tp=None):
            if tp is None:
                f = lambda e: e.matmul(out, lhsT, rhs, start=start, stop=stop)
            else:
                f = lambda e: e.matmul(out, lhsT, rhs, start=start, stop=stop, tile_position=tp)
            return P.op("pe", f, reads, writes)

        def act(out, in_, func, reads, writes, bias=None, scale=None):
            kw = {}
            if bias is not None:
                kw["bias"] = bias
            if scale is not None:
                kw["scale"] = scale
            return P.op("act", lambda e: e.activation(out=out, in_=in_, func=func, **kw), reads, writes)

        def tt(out, in0, in1, op, reads, writes, eng="dve"):
            return P.op(eng, lambda e: e.tensor_tensor(out=out, in0=in0, in1=in1, op=op), reads, writes)

        def ts(out, in0, s1, s2, op0, op1, reads, writes, eng="dve"):
            if op1 is None:
                f = lambda e: e.tensor_scalar(out=out, in0=in0, scalar1=s1, scalar2=None, op0=op0)
            else:
                f = lambda e: e.tensor_scalar(out=out, in0=in0, scalar1=s1, scalar2=s2, op0=op0, op1=op1)
            return P.op(eng, f, reads, writes)

        def stt(out, in0, scalar, in1, op0, op1, reads, writes, eng="dve"):
            return P.op(eng, lambda e: e.scalar_tensor_tensor(out=out, in0=in0, scalar=scalar, in1=in1,
                                                              op0=op0, op1=op1), reads, writes)

        def cp(out, in_, reads, writes, eng="dve"):
            return P.op(eng, lambda e: e.tensor_copy(out=out, in_=in_), reads, writes)

        def recip(out, in_, reads, writes):
            return P.op("dve", lambda e: e.reciprocal(out=out, in_=in_), reads, writes)

        sem_misc = [P.new_sem() for _ in range(4)]
        mi = [0]

        def msem():
            mi[0] += 1
            return sem_misc[mi[0] % 4]

        psrr = [0]

        def rstd_from_ps(ps_ap, n, inv_n, tag):
            ts(rstd[:, 0:n], ps_ap, inv_n, EPS, ALU.mult, ALU.add, [tag], ["rstd"])
            act(rstd[:, 0:n], rstd[:, 0:n], AF.Sqrt, ["rstd"], ["rstd"])
            recip(rstd[:, 0:n], rstd[:, 0:n], ["rstd"], ["rstd"])

        tmprr = [0]

        def norm_mod(xsrc, n, row, kG, kSH, out_fn, psb):
            for c0 in range(0, n, 256):
                cn = min(256, n - c0)
                for dc in range(DC):
                    xa, xr = xsrc(dc)
                    act(sq[:, dc, 0:cn], xa[:, c0:c0 + cn], AF.Square, [xr], ["sq%d" % dc])
                for dc in range(DC):
                    mm(PS[psb][:, c0:c0 + cn], ones_bf[:, :], sq[:, dc, 0:cn], dc == 0, dc == DC - 1,
                       ["sq%d" % dc, "ones"], ["ps%d" % psb])
            rstd_from_ps(PS[psb][:, 0:n], n, 1.0 / D, "ps%d" % psb)
            for dc in range(DC):
                xa, xr = xsrc(dc)
                t = tmprr[0] % 4
                tmprr[0] += 1
                tt(TMP[t][:, 0:n], xa, rstd[:, 0:n], ALU.mult, [xr, "rstd"], ["tmp%d" % t])
                oa, orr = out_fn(dc)
                act(oa, TMP[t][:, 0:n], AF.Identity, ["tmp%d" % t, "TAB"], [orr],
                    bias=TAB[:, row, kSH, dc:dc + 1], scale=TAB[:, row, kG, dc:dc + 1])

        s0 = P.new_sem()
        for (dst, src, nm) in [(cT_sb[:, :], cT.rearrange("p a b -> p (a b)"), "cT"),
                               (badaT_sb[:, :], b_adaT, "bada"), (badafT_sb[:, :], b_adafT, "badaf"),
                               (g3T_sb[:, :, :], g3T, "g3T"), (lam_sb[:, :], lamv, "lamv"),
                               (subg_sb[:, 0:1], subgT, "subg"), (pad_sb[:, :], padflag, "pad"),
                               (relb_sb[:, :], relb, "relb"), (ohp_sb[0:32, :], ohp, "XALL"),
                               (lnGB_sb[:, :, :], lnGB, "lnGB"), (bsB_sb[:, :, :], bsB, "bsB"),
                               (cmask_f, cmask, "XALL")]:
            dma("sp", dst, src, [], (XALL if nm == "XALL" else [nm]), s0)
        s0b = P.new_sem()
        dma("pool", wsT_sb[:, :, :], w_sT, [], ["wsT"], s0b)
        P.op("dve", lambda e: e.memset(ones_bf[:, :], 1.0), [], ["ones"])
        P.op("dve", lambda e: e.memset(wsT_sb[64:128, :, 0:64], 0.0), [], ["wsT"])
        act(siluT.rearrange("p a b -> p (a b)"), cT_sb[:, :], AF.Silu, ["cT"], ["siluT"])
        tt(TMP[0][:, 0:64], lam_sb[:, 0:64], lam_sb[:, 64:128], ALU.mult, ["lamv"], ["tmp0"])
        P.op("dve", lambda e: e.reduce_sum(out=lamt[:, 0:1], in_=TMP[0][:, 0:64], axis=AX.X), ["tmp0"], ["lamt"])
        tt(TMP[1][:, 0:64], lam_sb[:, 128:192], lam_sb[:, 192:256], ALU.mult, ["lamv"], ["tmp1"])
        P.op("dve", lambda e: e.reduce_sum(out=lamt[:, 1:2], in_=TMP[1][:, 0:64], axis=AX.X), ["tmp1"], ["lamt"])
        act(lamt[:, 2:4], lamt[:, 0:2], AF.Exp, ["lamt"], ["lamt2"])
        tt(lamt[:, 4:5], lamt[:, 3:4], lamt[:, 2:3], ALU.subtract, ["lamt2"], ["lamt3"])
        ts(lamt[:, 5:6], lamt[:, 4:5], -LAM_INIT, None, ALU.add, None, ["lamt3"], ["neglam"])
        neglam = lamt[:, 5:6]
        ts(subg_sb[:, 1:2], subg_sb[:, 0:1], 1.0 - LAM_INIT, None, ALU.mult, None, ["subg"], ["subg2"])
        subg2 = subg_sb[:, 1:2]

        for pi, (n0, nn) in enumerate([(0, 512), (512, 512), (1024, NG - 1024)]):
            mm(PS[pi][0:8, 0:nn], relb_sb[0:32, 0:8], ohp_sb[0:32, n0:n0 + nn], True, True,
               ["relb"] + XALL, ["ps%d" % pi])
            cp(g_sb[0:8, n0:n0 + nn], PS[pi][0:8, 0:nn], ["ps%d" % pi], ["g_sb"])
        sg = P.new_sem()
        dma("sp", Gd2[:, :], g_sb[0:8, :], ["g_sb"], ["Gd2"], sg)
        sbf = [P.new_sem(), P.new_sem()]
        for h in range(NH):
            b = h % 2
            src = bass.AP(Gd2.tensor, h * NG, [[1, 128], [1, 1024]])
            dma("sp", bfr_f[b], src, ["Gd2"], ["bfr%d" % b], sbf[b])
            rev = bass.AP(bfr_f[b].tensor, bfr_f[b].offset + 1023, [list(bfr_f[b].ap[0]), [-1, 1024]])
            tt(Bf[:, h, :], rev, cmask_f, ALU.add, ["bfr%d" % b] + XALL, ["Bf"])

        wsem = [P.new_sem(), P.new_sem()]
        wcnt = [0]

        def wload(src_ap, view=None, res=()):
            i = wcnt[0] % 2
            wcnt[0] += 1
            dst = WB[i][:, :, :] if view is None else view(WB[i])
            dma("pool", dst, src_ap, list(res), ["wb%d" % i], wsem[i])
            return i

        w_ada_v = w_ada.rearrange("(dc p) e -> p dc e", p=128)
        w_adaf_v = w_adaf.rearrange("(dc p) e -> p dc e", p=128)
        for (wv, npan, psb, Mdst, bsrc, nec) in [(w_ada_v, 24, 6, M1, badaT_sb, 96),
                                                 (w_adaf_v, 8, 7, MF, badafT_sb, 32)]:
            for pn in range(npan):
                i = wload(wv[:, :, pn * 512:(pn + 1) * 512])
                for cb in range(4):
                    ec = pn * 4 + cb
                    for dc in range(DC):
                        mm(PS[psb][:, ec * 3:ec * 3 + 3], WB[i][:, dc, cb * 128:(cb + 1) * 128],
                           siluT[:, dc, :], dc == 0, dc == DC - 1, ["wb%d" % i, "siluT"], ["ps%d" % psb])
            pv = PS[psb][:, 0:nec * 3].rearrange("p (a b) -> p a b", a=nec, b=3)
            for r in range(3):
                tt(Mdst[:, :, r], pv[:, :, r], bsrc[:, :], ALU.add, ["ps%d" % psb, "bada", "badaf"], ["Mada"])
        for r in range(3):
            for (kG, gsel, sc_ap) in [(0, 0, M1[:, 16:32, r]), (3, 1, M1[:, 64:80, r]), (6, 2, MF[:, 16:32, r])]:
                stt(TAB[:, r, kG, :], sc_ap, 1.0, g3T_sb[:, gsel, :], ALU.add, ALU.mult, ["Mada", "g3T"], ["TAB"])
            for (k, src_ap) in [(1, M1[:, 0:16, r]), (2, M1[:, 32:48, r]), (4, M1[:, 48:64, r]),
                                (5, M1[:, 80:96, r]), (7, MF[:, 0:16, r])]:
                cp(TAB[:, r, k, :], src_ap, ["Mada"], ["TAB"])

        w_in_v = w_in.rearrange("(dc p) e -> p dc e", p=128)
        swk = P.new_sem()
        for half in range(2):
            dma("pool", Wk[:, :, half * 512:(half + 1) * 512], w_in_v[:, :, 3072 + half * 512:3072 + (half + 1) * 512],
                ["Bf"], ["Wk"], swk)
            dma("pool", WB[half][:, :, :], w_in_v[:, :, 4096 + half * 512:4096 + (half + 1) * 512],
                ["Bf"], ["wb%d" % half], swk)
        xsT_v = xsT.rearrange("(dc p) t -> p dc t", p=128)
        xsem = [P.new_sem(), P.new_sem()]
        stsem = [P.new_sem() for _ in range(4)]
        osem = [P.new_sem() for _ in range(4)]
        stc = [0]
        oc = [0]

        def out_dma(dst, src, reads):
            s = osem[oc[0] % 4]
            oc[0] += 1
            o = dma("sp", dst, src, reads, [], s)
            P.out_ops.append(o)
            return o

        def load_x_halves(col0):
            for hf in range(2):
                dma("sp", XH[hf][:, :, :], xsT_v[:, :, col0 + hf * 256:col0 + (hf + 1) * 256], [],
                    ["xh%d_%d" % (hf, dc) for dc in range(DC)] + ["bfr0", "bfr1", "g_sb"], xsem[hf])

        psA = [0]

        def nextps(lst):
            psA[0] += 1
            return lst[psA[0] % len(lst)]

        w_out_v0 = w_out.rearrange("(dc p) e -> p dc e", p=128)
        w_fi_v0 = w_ffn_in.rearrange("(dc p) e -> p dc e", p=128)
        w_fo_v0 = w_ffn_out.rearrange("(fc p) e -> p fc e", p=128)
        csem = [P.new_sem(), P.new_sem()]
        cc = [0]

        def conv(dst, src, res):
            k = cc[0] % 2
            cc[0] += 1
            dma("pool", dst, src, [], [res], csem[k])
        if STOP >= 3:
            for pn in range(6):
                conv(WS_in[pn], w_in_v[:, :, pn * 512:(pn + 1) * 512], "ws_in%d" % pn)
            for pn in range(4):
                conv(WS_out[pn], w_out_v0[:, :, pn * 512:(pn + 1) * 512], "ws_out%d" % pn)
            for pn in range(22):
                conv(WS_fi[pn][:, :, 0:256], w_fi_v0[:, :, pn * 256:(pn + 1) * 256], "ws_fi%d" % pn)
                conv(WS_fi[pn][:, :, 256:512], w_fi_v0[:, :, DFF + pn * 256:DFF + (pn + 1) * 256], "ws_fi%d" % pn)
            for oc_ in range(DC):
                conv(WS_fo[oc_], w_fo_v0[:, :, oc_ * 128:(oc_ + 1) * 128], "ws_fo%d" % oc_)
            for pn in range(6, 10):
                conv(WS_in[pn], w_in_v[:, :, pn * 512:(pn + 1) * 512], "ws_in%d" % pn)

        hbuf = [hT, big(16384, 8192, (DC, 512))]
        hname = ["hT", "hU"]

        def norm_a_sq(hf):
            for dc in range(DC):
                act(sq[:, dc, 0:256], XH[hf][:, dc, :], AF.Square, ["xh%d_%d" % (hf, dc)], ["sq%d" % dc])

        def norm_a_rest(hf, hb):
            for dc in range(DC):
                mm(PS[7][:, 0:256], ones_bf[:, :], sq[:, dc, 0:256], dc == 0, dc == DC - 1,
                   ["sq%d" % dc, "ones"], ["ps7"])
            rstd_from_ps(PS[7][:, 0:256], 256, 1.0 / D, "ps7")
            for dc in range(DC):
                t = tmprr[0] % 4
                tmprr[0] += 1
                tt(TMP[t][:, 0:256], XH[hf][:, dc, :], rstd[:, 0:256], ALU.mult, ["xh%d_%d" % (hf, dc), "rstd"],
                   ["tmp%d" % t])
                act(hbuf[hb][:, dc, hf * 256:(hf + 1) * 256], TMP[t][:, 0:256], AF.Identity, ["tmp%d" % t, "TAB"],
                    ["%s%d_%d" % (hname[hb], hf, dc)], bias=TAB[:, 0, 1, dc:dc + 1], scale=TAB[:, 0, 0, dc:dc + 1])

        def load_x_half(col0, hf):
            dma("sp", XH[hf][:, :, :], xsT_v[:, :, col0 + hf * 256:col0 + (hf + 1) * 256], [],
                ["xh%d_%d" % (hf, dc) for dc in range(DC)] + ["bfr0", "bfr1", "g_sb"], xsem[hf])

        if NT > 0 and STOP >= 2:
            for hf in range(2):
                load_x_half(0, hf)
            for hf in range(2):
                norm_a_sq(hf)
                norm_a_rest(hf, 0)
                if NT > 1:
                    load_x_half(512, hf)
        for u in range(NT if STOP >= 2 else 0):
            own = (u % 4 == 3)
            slot = u // 4
            hb = u % 2
            hcur = hbuf[hb]
            hres = lambda dc, hb=hb: ["%s0_%d" % (hname[hb], dc), "%s1_%d" % (hname[hb], dc)]
            nxt = (u + 1 < NT)
            if nxt:
                norm_a_sq(0)
            for h in range(NH):
                pb = nextps([0, 1, 2, 3])
                for dc in range(DC):
                    mm(PS[pb][:, :], Wk[:, dc, h * 128:(h + 1) * 128], hcur[:, dc, :], dc == 0, dc == DC - 1,
                       ["Wk"] + hres(dc), ["ps%d" % pb])
                sbi = stc[0] % 2
                stc[0] += 1
                if own:
                    fi = stc[0] % 2
                    cp(STF[fi][:, :], PS[pb][:, :], ["ps%d" % pb], ["stf%d" % fi])
                    out_dma(kT_out[h, :, slot * 512:(slot + 1) * 512], STF[fi][:, :], ["stf%d" % fi])
                    act(STB[sbi][:, 0:512], STF[fi][:, :], AF.Copy, ["stf%d" % fi], ["stb%d" % sbi])
                else:
                    act(STB[sbi][:, 0:512], PS[pb][:, :], AF.Copy, ["ps%d" % pb], ["stb%d" % sbi])
                dma("sp", KT[h, :, u * 512:(u + 1) * 512], STB[sbi][:, 0:512], ["stb%d" % sbi], ["KT%d_%d" % (u, h)],
                    stsem[sbi])
                if h == 1 and nxt:
                    norm_a_rest(0, 1 - hb)
                    if u + 2 < NT:
                        load_x_half((u + 2) * 512, 0)
                if h == 3 and nxt:
                    norm_a_sq(1)
            for s in range(4):
                sbi = stc[0] % 2
                stc[0] += 1
                for cg in range(2):
                    pb = nextps([0, 1, 2, 3])
                    for dc in range(DC):
                        mm(PS[pb][:, :], hcur[:, dc, s * 128:(s + 1) * 128], WB[cg][:, dc, :],
                           dc == 0, dc == DC - 1, ["wb%d" % cg] + hres(dc), ["ps%d" % pb])
                    if own:
                        fi = (stc[0] + cg) % 2
                        cp(STF[fi][:, :], PS[pb][:, :], ["ps%d" % pb], ["stf%d" % fi])
                        out_dma(v_out[slot * 512 + s * 128:slot * 512 + (s + 1) * 128, cg * 512:(cg + 1) * 512],
                                STF[fi][:, :], ["stf%d" % fi])
                        act(STB[sbi][:, cg * 512:(cg + 1) * 512], STF[fi][:, :], AF.Copy, ["stf%d" % fi], ["stb%d" % sbi])
                    else:
                        act(STB[sbi][:, cg * 512:(cg + 1) * 512], PS[pb][:, :], AF.Copy, ["ps%d" % pb], ["stb%d" % sbi])
                dma("sp", VB[:, :, u * 4 + s, :].rearrange("h p d -> p h d"),
                    STB[sbi][:, :].rearrange("p (h d) -> p h d", h=NH, d=128), ["stb%d" % sbi], ["VB%d_%d" % (u, s)],
                    stsem[2 + sbi])
                if s == 0 and nxt:
                    norm_a_rest(1, 1 - hb)
                    if u + 2 < NT:
                        load_x_half((u + 2) * 512, 1)

        kvsem = [P.new_sem(), P.new_sem()]
        kvsem_s = [P.new_sem(), P.new_sem()]
        kvc = [0]
        w_out_v = w_out.rearrange("(dc p) e -> p dc e", p=128)
        w_fi_v = w_ffn_in.rearrange("(dc p) e -> p dc e", p=128)
        w_fo_v = w_ffn_out.rearrange("(fc p) e -> p fc e", p=128)

        def attn_run(heads):
            flat = [(hi, ci) for hi, hd in enumerate(heads) for ci in range(len(hd["chunks"]))]
            loaded = set()

            def load(k):
                if k < len(flat) and k not in loaded:
                    loaded.add(k)
                    hi, ci = flat[k]
                    ld = heads[hi]["chunks"][ci][0]
                    if ld is not None:
                        ld()
            load(0)
            fk = 0
            pending = []
            for hi, hd in enumerate(heads):
                q_ap, nq, qres = hd["q_ap"], hd["nq"], hd["qres"]
                blks = []
                for ci, (ld, blocks) in enumerate(hd["chunks"]):
                    for bj, blk in enumerate(blocks):
                        blks.append((blk, fk + ci if bj == 0 else None))
                nblk = len(blks)

                def emit_qk(b):
                    (kT_ap, v_ap, nk, bias_ap, flag_ap, kres, vres), pk = blks[b]
                    if pk is not None:
                        load(pk)
                    sp_ = b % 2
                    mm(PSS[sp_][0:nk, 0:nq], kT_ap[0:64, :], q_ap[0:64, :], True, True, [kres, qres],
                       ["pss%d" % sp_], tp=(0, 0))
                    mm(PSS[sp_][0:nk, 512:512 + nq], kT_ap[64:128, :], q_ap[64:128, :], True, True, [kres, qres],
                       ["pss%d" % sp_], tp=(64, 0))

                obase = 4
                accp = hi % 2
                use_pool = (nq == 512)

                def emit_rest(b):
                    (kT_ap, v_ap, nk, bias_ap, flag_ap, kres, vres), pk = blks[b]
                    if pk is not None:
                        load(pk + 1)
                    sp_ = b % 2
                    first = (b == 0)
                    last = (b == nblk - 1)
                    pres = "ptp%d" % sp_
                    if nq == 512:
                        src = PSS[sp_][0:nk, :]
                        dst = PTP[sp_][0:nk, :]
                    else:
                        src = PSS[sp_][0:nk, :].rearrange("p (t n) -> p t n", t=2, n=512)[:, :, 0:nq]
                        dst = PTP[sp_][0:nk, :].rearrange("p (t n) -> p t n", t=2, n=512)[:, :, 0:nq]
                    rd = ["pss%d" % sp_]
                    if bias_ap is not None:
                        for t in range(2):
                            tmi = 2 * sp_ + t
                            tt(TMP[tmi][0:nk, 0:nq], PSS[sp_][0:nk, t * 512:t * 512 + nq], bias_ap, ALU.add,
                               ["pss%d" % sp_, "Bf"], ["tmp%d" % tmi])
                            kw = {}
                            if flag_ap is not None:
                                kw["bias"] = flag_ap[0:nk, :]
                            act(PTP[sp_][0:nk, t * 512:t * 512 + nq], TMP[tmi][0:nk, 0:nq], AF.Exp,
                                ["tmp%d" % tmi, "pad"], [pres], **kw)
                    elif flag_ap is not None:
                        act(dst, src, AF.Exp, rd + ["pad"], [pres], bias=flag_ap[0:nk, :])
                    else:
                        act(dst, src, AF.Exp, rd, [pres])
                    for t in range(2):
                        pt = PTP[sp_][0:nk, t * 512:t * 512 + nq]
                        mm(PS[obase + t][:, 0:nq], v_ap, pt, first, last, [vres, pres], ["ps%d" % (obase + t)])
                    pt0 = PTP[sp_][0:nk, 0:nq]
                    pt1 = PTP[sp_][0:nk, 512:512 + nq]
                    ac = ACC[0:nk, accp, 0:nq]
                    ares = ACCRES[accp]
                    if first:
                        cp(ac, pt0, [pres], ares)
                    else:
                        tt(ac, ac, pt0, ALU.add, [pres] + ares, ares)
                    mm(PS[6][:, 0:nq], ones_bf[0:nk, :], pt1, first, last, ["ones", pres], ["ps6"])

                def fin_stage1(nq=nq, accp=accp):
                    cp(TMP[0][:, 0:nq], PS[4][:, 0:nq], ["ps4"], ["tmp0"])
                    cp(TMP[1][:, 0:nq], PS[5][:, 0:nq], ["ps5"], ["tmp1"])
                    cp(TMP[3][:, 0:nq], PS[6][:, 0:nq], ["ps6"], ["tmp3"])
                    cp(ACCB[:, 0, 0:nq], ACC[:, accp, 0:nq], ACCRES[accp], ["accb0"])

                def fin_stage2(nq=nq):
                    mm(PS[7][:, 0:nq], ones_bf[:, :], ACCB[:, 0, 0:nq], True, True, ["ones", "accb0"], ["ps7"])
                    recip(TMP[2][:, 0:nq], PS[7][:, 0:nq], ["ps7"], ["tmp2"])
                    tt(TMP[0][:, 0:nq], TMP[0][:, 0:nq], TMP[2][:, 0:nq], ALU.mult, ["tmp0", "tmp2"], ["tmp0"])
                    recip(TMP[3][:, 0:nq], TMP[3][:, 0:nq], ["tmp3"], ["tmp3"])
                    tt(TMP[1][:, 0:nq], TMP[1][:, 0:nq], TMP[3][:, 0:nq], ALU.mult, ["tmp1", "tmp3"], ["tmp1"])
                    stt(TMP[0][:, 0:nq], TMP[1][:, 0:nq], neglam, TMP[0][:, 0:nq], ALU.mult, ALU.add,
                        ["tmp0", "tmp1", "neglam"], ["tmp0"])

                def fin_stage2b(nq=nq):
                    act(ACCB[:, 1, 0:nq], TMP[0][:, 0:nq], AF.Square, ["tmp0"], ["accb1"])

                def fin_stage3(out_ap=hd["out_ap"], out_res=hd["out_res"], nq=nq, last_head=(hi == len(heads) - 1)):
                    pb = 7
                    mm(PS[pb][:, 0:nq], ones_bf[:, :], ACCB[:, 1, 0:nq], True, True, ["ones", "accb1"], ["ps%d" % pb])
                    rstd_from_ps(PS[pb][:, 0:nq], nq, 1.0 / 128, "ps%d" % pb)
                    tt(TMP[0][:, 0:nq], TMP[0][:, 0:nq], rstd[:, 0:nq], ALU.mult, ["tmp0", "rstd"], ["tmp0"])
                    act(out_ap, TMP[0][:, 0:nq], AF.Identity, ["tmp0", "subg2"], [out_res], scale=subg2)

                emit_qk(0)
                for b in range(nblk):
                    if b + 1 < nblk:
                        emit_qk(b + 1)
                    emit_rest(b)
                    if pending and b == 1:
                        pending[0]()
                    if pending and b == 6:
                        pending[1]()
                    if pending and b == 10:
                        pending[2]()
                        del pending[:]
                fk += len(hd["chunks"])
                fin_stage1()
                pending.extend([fin_stage2, fin_stage2b, fin_stage3])
            if pending:
                pending[0]()
                pending[1]()
                pending[2]()
                del pending[:]

        def do_slot(slot, sample):
            NTOK = 128 if sample else 512
            if sample:
                subt = [(0, 64, 1), (64, 64, 2)]
                colr = [(0, 64, 1), (64, 64, 2)]
            else:
                subt = [(i * 128, 128, 0) for i in range(4)]
                colr = [(0, 512, 0)]
            if sample:
                dma("sp", XH[0][:, :, 0:128], xsmpT.rearrange("(dc p) t -> p dc t", p=128), [],
                    ["xh0_%d" % dc for dc in range(DC)], xsem[0])
            else:
                load_x_halves((4 * slot + 3) * 512)

            def xpieces(c0, n):
                out = []
                for hf in range(2):
                    a = max(c0, hf * 256)
                    b = min(c0 + n, (hf + 1) * 256)
                    if b > a:
                        out.append((hf, a - hf * 256, b - a, a))
                return out

            def do_norm(kG, kSH, fp32_out=None):
                for (c0, n, row) in colr:
                    for (hf, lc, ln_, gc) in xpieces(c0, n):
                        if fp32_out is None:
                            ofn = lambda dc, hf=hf, gc=gc, ln_=ln_: (hT[:, dc, gc:gc + ln_], "hT%d_%d" % (hf, dc))
                        else:
                            ofn = fp32_out(hf, lc, ln_, gc)
                        norm_mod(lambda dc, hf=hf, lc=lc, ln_=ln_: (XH[hf][:, dc, lc:lc + ln_], "xh%d_%d" % (hf, dc)),
                                 ln_, row, kG, kSH, ofn, 7)

            do_norm(0, 1)
            hres = (lambda dc: ["hT0_%d" % dc]) if sample else (lambda dc: ["hT0_%d" % dc, "hT1_%d" % dc])

            def gv_ln_gate(si, c0, n, gb):
                g = gvf[0:n, gb, :]
                gr = "gvf%d" % gb
                gbr = "gvb%d" % gb
                P.op("dve", lambda e: e.reduce_sum(out=small[0:n, 0:1], in_=g, axis=AX.X), [gr], ["small"])
                ts(small[0:n, 1:2], small[0:n, 0:1], -1.0 / 1024, None, ALU.mult, None, ["small"], ["small"])
                ts(g, g, small[0:n, 1:2], None, ALU.add, None, [gr, "small"], [gr])
                tt(gvb[0:n, gb, :], g, g, ALU.mult, [gr], [gbr])
                P.op("dve", lambda e: e.reduce_sum(out=small[0:n, 2:3], in_=gvb[0:n, gb, :], axis=AX.X),
                     [gbr], ["small"])
                ts(small[0:n, 3:4], small[0:n, 2:3], 1.0 / 1024, EPS, ALU.mult, ALU.add, ["small"], ["small"])
                act(small[0:n, 3:4], small[0:n, 3:4], AF.Sqrt, ["small"], ["small"])
                recip(small[0:n, 4:5], small[0:n, 3:4], ["small"], ["small"])
                stt(g, g, small[0:n, 4:5], lnGB_sb[0:n, 0, :], ALU.mult, ALU.mult, [gr, "small", "lnGB"], [gr])
                tt(g, g, lnGB_sb[0:n, 1, :], ALU.add, [gr, "lnGB"], [gr])
                cp(gvb[0:n, gb, :], g, [gr], [gbr])
                if sample:
                    out_dma(gvs_out[c0:c0 + n, :], g, [gr])
                for g0 in (0, 4):
                    pb = nextps([0, 1, 2, 3])
                    for gg in range(4):
                        gi = g0 + gg
                        mm(PS[pb][:, gg * n:(gg + 1) * n], gvb[0:n, gb, gi * 128:(gi + 1) * 128],
                           wsT_sb[0:n, gi, 0:n], True, True, [gbr, "wsT"], ["ps%d" % pb])
                    t = tmprr[0] % 4
                    tmprr[0] += 1
                    pv3 = PS[pb][:, 0:4 * n].rearrange("p (a b) -> p a b", a=4, b=n)
                    tv3 = TMP[t][:, 0:4 * n].rearrange("p (a b) -> p a b", a=4, b=n)
                    tt(tv3, pv3, bsB_sb[:, g0:g0 + 4, 0:n], ALU.add, ["ps%d" % pb, "bsB"], ["tmp%d" % t])
                    tt(catT[:, g0:g0 + 4, c0:c0 + n], tv3, uT[:, g0:g0 + 4, c0:c0 + n], ALU.mult,
                       ["tmp%d" % t] + ["uT%d" % q for q in range(g0, g0 + 4)],
                       ["cat%d" % q for q in range(g0, g0 + 4)])

            npan = 10 if sample else 6
            for pn in range(npan):
                i = wload(WS_in[pn], res=["ws_in%d" % pn])
                wr = "wb%d" % i
                if pn < 2:
                    for cb in range(4):
                        ec = pn * 4 + cb
                        pb = nextps([0, 1, 2, 3])
                        for dc in range(DC):
                            mm(PS[pb][:, 0:NTOK], WB[i][:, dc, cb * 128:(cb + 1) * 128], hT[:, dc, 0:NTOK],
                               dc == 0, dc == DC - 1, [wr] + hres(dc), ["ps%d" % pb])
                        act(uT[:, ec, 0:NTOK], PS[pb][:, 0:NTOK], AF.Gelu_apprx_tanh, ["ps%d" % pb], ["uT%d" % ec])
                elif pn == 2:
                    i2 = i
                    continue
                elif pn == 3:
                    i3 = i
                    for si, (c0, n, row) in enumerate(subt):
                        gb = 0
                        for cg, iw in enumerate((i2, i3)):
                            pb = nextps([0, 1, 2, 3])
                            for dc in range(DC):
                                mm(PS[pb][0:n, :], hT[:, dc, c0:c0 + n], WB[iw][:, dc, :], dc == 0, dc == DC - 1,
                                   ["wb%d" % iw] + hres(dc), ["ps%d" % pb])
                            act(gvf[0:n, gb, cg * 512:(cg + 1) * 512], PS[pb][0:n, :], AF.Gelu_apprx_tanh,
                                ["ps%d" % pb], ["gvf%d" % gb])
                        gv_ln_gate(si, c0, n, gb)
                elif pn < 6:
                    for cb in range(4):
                        hh = (pn - 4) * 4 + cb
                        pb = nextps([0, 1, 2, 3])
                        for dc in range(DC):
                            mm(PS[pb][:, 0:NTOK], WB[i][:, dc, cb * 128:(cb + 1) * 128], hT[:, dc, 0:NTOK],
                               dc == 0, dc == DC - 1, [wr] + hres(dc), ["ps%d" % pb])
                        act(QT[:, hh, 0:NTOK], PS[pb][:, 0:NTOK], AF.Copy, ["ps%d" % pb], ["QT%d" % hh], scale=0.125)
                elif pn < 8:
                    for cb in range(4):
                        hh = (pn - 6) * 4 + cb
                        pb = nextps([0, 1, 2, 3])
                        for dc in range(DC):
                            mm(PS[pb][:, 0:NTOK], WB[i][:, dc, cb * 128:(cb + 1) * 128], hT[:, dc, 0:NTOK],
                               dc == 0, dc == DC - 1, [wr] + hres(dc), ["ps%d" % pb])
                        fi = hh % 2
                        cp(STF[fi][:, 0:NTOK], PS[pb][:, 0:NTOK], ["ps%d" % pb], ["stf%d" % fi])
                        out_dma(ksT_out[hh, :, :], STF[fi][:, 0:NTOK], ["stf%d" % fi])
                        act(ksT[:, hh, 0:NTOK], STF[fi][:, 0:NTOK], AF.Copy, ["stf%d" % fi], ["ksT%d" % hh])
                else:
                    for si, (c0, n, row) in enumerate(subt):
                        pb = nextps([0, 1, 2, 3])
                        for dc in range(DC):
                            mm(PS[pb][0:n, :], hT[:, dc, c0:c0 + n], WB[i][:, dc, :], dc == 0, dc == DC - 1,
                               [wr] + hres(dc), ["ps%d" % pb])
                        fi = (si + pn) % 2
                        cp(STF[fi][0:n, :], PS[pb][0:n, :], ["ps%d" % pb], ["stf%d" % fi])
                        out_dma(vs_out[c0:c0 + n, (pn - 8) * 512:(pn - 7) * 512], STF[fi][0:n, :], ["stf%d" % fi])
                        act(vsb[0:n, si, (pn - 8) * 512:(pn - 7) * 512], STF[fi][0:n, :], AF.Copy, ["stf%d" % fi],
                            ["vsb%d" % si])

            heads = []
            for h in range(NH):
                if not sample:
                    chunks = []
                    for c in range(slot + 1):
                        bsel = kvc[0] % 2
                        kvc[0] += 1

                        def loader(c=c, bsel=bsel, h=h):
                            dma("sp", KBUF[bsel], KT[h, :, c * 2048:(c + 1) * 2048],
                                ["KT%d_%d" % (uu, h) for uu in range(4 * c, 4 * c + 4)], ["kbuf%d" % bsel], kvsem[bsel])
                            dma("sp", VBUF[bsel], VB[h, :, c * 16:(c + 1) * 16, :],
                                ["VB%d_%d" % (uu, ss) for uu in range(4 * c, 4 * c + 4) for ss in range(4)],
                                ["vbuf%d" % bsel], kvsem[bsel])
                        blocks = []
                        for kb in range(16):
                            t512 = c * 4 + kb // 4
                            bias_ap = None
                            if c == slot and kb >= 11:
                                r = kb - 12
                                off = 384 - 128 * r
                                bias_ap = Bf[:, h, off:off + 512]
                            flag_ap = pad_sb[:, t512:t512 + 1] if t512 < 3 else None
                            blocks.append((KBUF[bsel][:, kb * 128:(kb + 1) * 128], VBUF[bsel][:, kb, :], 128, bias_ap,
                                           flag_ap, "kbuf%d" % bsel, "vbuf%d" % bsel))
                        chunks.append((loader, blocks))
                    heads.append(dict(q_ap=QT[:, h, :], nq=512, chunks=chunks, out_ap=catT[:, 8 + h, :],
                                      out_res="cat%d" % (8 + h), qres="QT%d" % h))
                else:
                    for st in range(2):
                        bsel = kvc[0] % 2
                        kvc[0] += 1

                        def loader(st=st, bsel=bsel, h=h):
                            dma("pool", KBUF[bsel], ckT[st, h, :, :], [], ["kbuf%d" % bsel], kvsem_s[bsel])
                            dma("pool", VBUF[bsel], cvB[st, h, :, :, :], [], ["vbuf%d" % bsel], kvsem_s[bsel])
                        blocks = []
                        for kb in range(16):
                            bias_ap = Bf[:, h, 512:576] if kb == 15 else None
                            blocks.append((KBUF[bsel][:, kb * 128:(kb + 1) * 128], VBUF[bsel][:, kb, :], 128, bias_ap,
                                           None, "kbuf%d" % bsel, "vbuf%d" % bsel))
                        blocks.append((ksT[:, h, st * 64:(st + 1) * 64], vsb[0:64, st, h * 128:(h + 1) * 128], 64,
                                       Bf[0:64, h, 384:448], None, "ksT%d" % h, "vsb%d" % st))
                        heads.append(dict(q_ap=QT[:, h, st * 64:(st + 1) * 64], nq=64, chunks=[(loader, blocks)],
                                          out_ap=catT[:, 8 + h, st * 64:(st + 1) * 64], out_res="cat%d" % (8 + h),
                                          qres="QT%d" % h))
            attn_run(heads)

            for pn in range(4):
                i = wload(WS_out[pn], res=["ws_out%d" % pn])
                for cb in range(4):
                    oc_ = pn * 4 + cb
                    pb = nextps([0, 1, 2, 3])
                    for ec in range(DC):
                        mm(PS[pb][:, 0:NTOK], WB[i][:, ec, cb * 128:(cb + 1) * 128], catT[:, ec, 0:NTOK],
                           ec == 0, ec == DC - 1, ["wb%d" % i, "cat%d" % ec], ["ps%d" % pb])
                    for (c0, n, row) in colr:
                        for (hf, lc, ln_, gc) in xpieces(c0, n):
                            stt(XH[hf][:, oc_, lc:lc + ln_], PS[pb][:, gc:gc + ln_], TAB[:, row, 2, oc_:oc_ + 1],
                                XH[hf][:, oc_, lc:lc + ln_], ALU.mult, ALU.add,
                                ["ps%d" % pb, "TAB", "xh%d_%d" % (hf, oc_)], ["xh%d_%d" % (hf, oc_)])

            do_norm(3, 4)
            for pn in range(22):
                i = wload(WS_fi[pn], res=["ws_fi%d" % pn])
                for fl in range(2):
                    fc = pn * 2 + fl
                    pg = nextps([0, 1, 2, 3])
                    for dc in range(DC):
                        mm(PS[pg][:, 0:NTOK], WB[i][:, dc, fl * 128:(fl + 1) * 128], hT[:, dc, 0:NTOK],
                           dc == 0, dc == DC - 1, ["wb%d" % i] + hres(dc), ["ps%d" % pg])
                    pu = nextps([0, 1, 2, 3])
                    for dc in range(DC):
                        mm(PS[pu][:, 0:NTOK], WB[i][:, dc, 256 + fl * 128:256 + (fl + 1) * 128], hT[:, dc, 0:NTOK],
                           dc == 0, dc == DC - 1, ["wb%d" % i] + hres(dc), ["ps%d" % pu])
                    t = tmprr[0] % 4
                    tmprr[0] += 1
                    act(TMP[t][:, 0:NTOK], PS[pg][:, 0:NTOK], AF.Silu, ["ps%d" % pg], ["tmp%d" % t])
                    tt(actT[:, fc, 0:NTOK], TMP[t][:, 0:NTOK], PS[pu][:, 0:NTOK], ALU.mult,
                       ["tmp%d" % t, "ps%d" % pu], ["actT%d" % fc])
            for oc_ in range(DC):
                i = wcnt[0] % 2
                wcnt[0] += 1
                wview = WB[i].rearrange("p a b -> p (a b)")[:, 0:FC * 128].rearrange("p (f c) -> p f c", f=FC, c=128)
                dma("pool", wview, WS_fo[oc_], ["ws_fo%d" % oc_], ["wb%d" % i], wsem[i])
                pb = nextps([0, 1, 2, 3])
                for fc in range(FC):
                    lastmm = mm(PS[pb][:, 0:NTOK], wview[:, fc, :], actT[:, fc, 0:NTOK], fc == 0, fc == FC - 1,
                                ["wb%d" % i, "actT%d" % fc], ["ps%d" % pb])
                if oc_ == DC - 1:
                    P.add_writes(lastmm, ["kbuf0", "kbuf1", "vbuf0", "vbuf1"])
                for (c0, n, row) in colr:
                    for (hf, lc, ln_, gc) in xpieces(c0, n):
                        stt(XH[hf][:, oc_, lc:lc + ln_], PS[pb][:, gc:gc + ln_], TAB[:, row, 5, oc_:oc_ + 1],
                            XH[hf][:, oc_, lc:lc + ln_], ALU.mult, ALU.add,
                            ["ps%d" % pb, "TAB", "xh%d_%d" % (hf, oc_)], ["xh%d_%d" % (hf, oc_)])

            ydst = ysT if sample else yT
            ycol0 = 0 if sample else slot * 512
            ydst_v = ydst.rearrange("(dc p) t -> p dc t", p=128)
            fo = [0]

            def fin_out(hf, lc, ln_, gc):
                def f(dc):
                    fi = fo[0] % 2
                    return (STF[fi][:, 0:ln_], "stf%d" % fi)
                return f
            for (c0, n, row) in colr:
                for (hf, lc, ln_, gc) in xpieces(c0, n):
                    for dc in range(DC):
                        act(sq[:, dc, 0:ln_], XH[hf][:, dc, lc:lc + ln_], AF.Square, ["xh%d_%d" % (hf, dc)],
                            ["sq%d" % dc])
                    for dc in range(DC):
                        mm(PS[7][:, 0:ln_], ones_bf[:, :], sq[:, dc, 0:ln_], dc == 0, dc == DC - 1,
                           ["sq%d" % dc, "ones"], ["ps7"])
                    rstd_from_ps(PS[7][:, 0:ln_], ln_, 1.0 / D, "ps7")
                    for dc in range(DC):
                        t = tmprr[0] % 4
                        tmprr[0] += 1
                        tt(TMP[t][:, 0:ln_], XH[hf][:, dc, lc:lc + ln_], rstd[:, 0:ln_], ALU.mult,
                           ["xh%d_%d" % (hf, dc), "rstd"], ["tmp%d" % t])
                        fi = fo[0] % 2
                        fo[0] += 1
                        act(STF[fi][:, 0:ln_], TMP[t][:, 0:ln_], AF.Identity, ["tmp%d" % t, "TAB"], ["stf%d" % fi],
                            bias=TAB[:, row, 7, dc:dc + 1], scale=TAB[:, row, 6, dc:dc + 1])
                        out_dma(ydst_v[:, dc, ycol0 + gc:ycol0 + gc + ln_], STF[fi][:, 0:ln_], ["stf%d" % fi])

        P.add_writes(P.ops["pe"][-1], ["kbuf0", "kbuf1", "vbuf0", "vbuf1"])
        for slot in range(NSLOT if STOP >= 3 else 0):
            do_slot(slot, False)
        if STOP >= 4:
            do_slot(0, True)

        P.op("sp", lambda e: e.nop(), [], [], None)
        fin = P.ops["sp"][-1]
        fin.deps = [(o, True) for o in P.out_ops]

        P.finalize()
        if _os.environ.get('SEMDBG'):
            print('SEMDBG', {e: P.eng_sem[e].total for e in P.eng_sem}, 'ninstr', {e: len(P.ops[e]) for e in P.ops})
        with nc.Block() as block:
            @block.sync
            def _(e):
                P.emit("sp", e)

            @block.gpsimd
            def _(e):
                P.emit("pool", e)

            @block.tensor
            def _(e):
                P.emit("pe", e)

            @block.scalar
            def _(e):
                P.emit("act", e)

            @block.vector
            def _(e):
                P.emit("dve", e)
    return nc


def _t5_bucket(rel):
    nb = 16
    max_exact = 8
    ret = np.where(rel > 0, nb, 0)
    n = np.abs(rel)
    nf = np.maximum(n, 1).astype(np.float32)
    large = max_exact + (np.log(nf / max_exact) / math.log(128 / max_exact) * (nb - max_exact)).astype(np.int32)
    large = np.minimum(large, nb - 1)
    return ret + np.where(n < max_exact, n, large)


def _consts():
    d = np.arange(NG) - 639
    bk = _t5_bucket(d.astype(np.int64))
    oh = np.zeros((32, NG), np.float32)
    oh[bk, np.arange(NG)] = 1.0
    oh[15, :] -= 1.0
    p = np.arange(128)[:, None]
    m = np.arange(1024)[None, :] - 384
    cm = np.where((p // 64) <= np.floor_divide(m, 64), 0.0, NEG).astype(np.float32)
    return oh, cm


_NC_CACHE = {}


def kernel(x_prompt, x_sample, cache_k, cache_v, c_prompt, c_sample, rel_bias,
           w_ada, b_ada, w_ada_final, b_ada_final, g_mix, g_ffn, g_final,
           w_in, mlp_ln_g, mlp_ln_b, w_s, b_s, lambda_q1, lambda_k1, lambda_q2, lambda_k2,
           sub_g, w_out, w_ffn_in, w_ffn_out):
    f = np.float32
    A = lambda a: np.ascontiguousarray(np.asarray(a, dtype=f))
    x_prompt = A(x_prompt); x_sample = A(x_sample); cache_k = A(cache_k); cache_v = A(cache_v)
    B, S, _ = x_prompt.shape
    NT = S // 512
    NSLOT = NT // 4
    assert NT % 4 == 0 and B == 2
    if NT not in _NC_CACHE:
        _NC_CACHE[NT] = build_nc(NT)
    nc = _NC_CACHE[NT]
    oh, cm = _consts()

    def fm(v, n):
        return A(np.asarray(v, f).reshape(n, 128).T)

    shared = {
        "w_ada": A(w_ada)[0], "b_adaT": fm(np.asarray(b_ada)[0], 96),
        "w_adaf": A(w_ada_final), "b_adafT": fm(b_ada_final, 32),
        "g3T": A(np.stack([fm(np.asarray(g_mix)[0], 16), fm(np.asarray(g_ffn)[0], 16), fm(g_final, 16)], axis=1)),
        "w_in": A(w_in)[0],
        "lnGB": A(np.broadcast_to(np.stack([np.asarray(mlp_ln_g, f)[0], np.asarray(mlp_ln_b, f)[0]])[None], (128, 2, 1024))),
        "w_sT": A(np.transpose(np.asarray(w_s, f)[0], (2, 0, 1))),
        "bsB": A(np.broadcast_to(np.asarray(b_s, f)[0][None], (128, 8, 128))),
        "lamv": A(np.broadcast_to(np.concatenate([np.asarray(v, f)[0] for v in
                                                  (lambda_q1, lambda_k1, lambda_q2, lambda_k2)])[None], (128, 256))),
        "subgT": A(np.asarray(sub_g, f)[0].reshape(128, 1)),
        "relb": A(rel_bias), "ohp": oh, "cmask": cm,
        "w_out": A(w_out)[0], "w_ffn_in": A(w_ffn_in)[0], "w_ffn_out": A(w_ffn_out)[0],
    }
    in_maps = []
    for c in range(8):
        b, j = c // 4, c % 4
        pad = (3 - j) * 512
        xs = np.zeros((NT * 512, D), f)
        nreal = NT * 512 - pad
        xs[pad:] = x_prompt[b, :nreal]
        pf = np.zeros((128, 4), f)
        for r in range(3):
            if r < 3 - j:
                pf[:, r] = NEG
        s0 = 2 * c
        xsm = np.concatenate([x_sample[s0], x_sample[s0 + 1]], axis=0)
        ck = np.stack([np.transpose(cache_k[0, s0 + st], (1, 2, 0)) for st in range(2)])
        cv = np.stack([np.transpose(cache_v[0, s0 + st].reshape(16, 128, NH, 128), (2, 1, 0, 3)) for st in range(2)])
        crow = np.stack([np.asarray(c_prompt, f)[b], np.asarray(c_sample, f)[s0], np.asarray(c_sample, f)[s0 + 1]])
        cTa = np.transpose(crow.reshape(3, DC, 128), (2, 1, 0))
        m = dict(shared)
        m.update({"xsT": A(xs.T), "xsmpT": A(xsm.T), "ckT": A(ck), "cvB": A(cv), "cT": A(cTa), "padflag": pf})
        in_maps.append(m)

    if KCORES < 8:
        res = run_bass_kernel_spmd(nc, in_maps[:KCORES], core_ids=list(range(KCORES)))
        R = list(res.results)
        R = R + [{k: np.zeros_like(v) for k, v in R[0].items()} for _ in range(8 - KCORES)]
    else:
        res = run_bass_kernel_spmd(nc, in_maps, core_ids=list(range(8)))
        R = res.results
    DB, T = x_sample.shape[0], x_sample.shape[1]
    y_p = np.zeros((B, S, D), f)
    y_s = np.zeros((DB, T, D), f)
    nk_p = np.zeros((1, B, S, NH, 128), f)
    nv_p = np.zeros((1, B, S, NH, 128), f)
    nk_s = np.zeros((1, DB, T, NH, 128), f)
    nv_s = np.zeros((1, DB, T, NH, 128), f)
    ngv_s = np.zeros((1, DB, T, 8, 128), f)
    for c in range(8):
        b, j = c // 4, c % 4
        r = R[c]
        for i in range(NSLOT):
            p0 = (4 * i + j) * 512
            y_p[b, p0:p0 + 512] = r["yT"][:, i * 512:(i + 1) * 512].T
            nk_p[0, b, p0:p0 + 512] = np.transpose(r["kT_out"][:, :, i * 512:(i + 1) * 512], (2, 0, 1))
            nv_p[0, b, p0:p0 + 512] = r["v_out"][i * 512:(i + 1) * 512].reshape(512, NH, 128)
        for st in range(2):
            s = 2 * c + st
            y_s[s] = r["ysT"][:, st * 64:(st + 1) * 64].T
            nk_s[0, s] = np.transpose(r["ksT_out"][:, :, st * 64:(st + 1) * 64], (2, 0, 1))
            nv_s[0, s] = r["vs_out"][st * 64:(st + 1) * 64].reshape(64, NH, 128)
            ngv_s[0, s] = r["gvs_out"][st * 64:(st + 1) * 64].reshape(64, 8, 128)
    return (y_p, y_s, nk_p, nv_p, nk_s, nv_s, ngv_s)
```

```python
import math
from contextlib import ExitStack

import numpy as np
import concourse.bass as bass
import concourse.mybir as mybir
from concourse.bass_utils import run_bass_kernel_spmd

F32 = mybir.dt.float32
BF16 = mybir.dt.bfloat16
AF = mybir.ActivationFunctionType
ALU = mybir.AluOpType
AX = mybir.AxisListType

D = 2048
DC = 16
NH = 8
DFF = 5632
FC = 44
EPS = 1e-6
PAST = 2048
NEG = -30000.0
LAM_INIT = 0.8 - 0.6 * math.exp(-0.3 * 0)
NG = 1152
SAME_ENG_SYNC = True
import os as _os
STOP = int(_os.environ.get('KSTOP', '99'))
KSUB = int(_os.environ.get('KSUB', '99'))
KCORES = int(_os.environ.get('KCORES', '8'))
KVAR = _os.environ.get('KVAR', 'ABCD')


class Sem:
    def __init__(self, h):
        self.h = h
        self.total = 0


class Op:
    __slots__ = ("eng", "fn", "deps", "dma_sem", "token", "need", "is_dma", "ninst")

    def __init__(self, eng, fn, dma_sem=None, ninst=1):
        self.eng = eng
        self.fn = fn
        self.deps = []
        self.dma_sem = dma_sem
        self.is_dma = dma_sem is not None
        self.token = None
        self.need = False
        self.ninst = ninst


class Prog:
    ENGS = ("sp", "pool", "pe", "act", "dve")

    def __init__(self, nc, es):
        self.nc = nc
        self.es = es
        self.ops = {e: [] for e in self.ENGS}
        self.res = {}
        self.eng_sem = {}
        for e in ("pool", "pe", "act", "dve"):
            self.eng_sem[e] = Sem(es.enter_context(nc.semaphore("es_" + e)))
        self.nsem = 0
        self.out_ops = []

    def new_sem(self):
        self.nsem += 1
        return Sem(self.es.enter_context(self.nc.semaphore("ds%d" % self.nsem)))

    RENAME = {"ps0": "pss0", "ps1": "pss0", "ps2": "pss1", "ps3": "pss1", "accb0": "gvf0", "accb1": "gvf0"}

    def op(self, eng, fn, reads=(), writes=(), dma_sem=None, ninst=1):
        reads = [self.RENAME.get(r, r) for r in reads]
        writes = [self.RENAME.get(w, w) for w in writes]
        o = Op(eng, fn, dma_sem, ninst)
        deps = {}
        for r in reads:
            st = self.res.get(r)
            if st is not None and st[0] is not None:
                deps[id(st[0])] = (st[0], True)
        for w in writes:
            st = self.res.get(w)
            if st is not None:
                if st[0] is not None and id(st[0]) not in deps:
                    deps[id(st[0])] = (st[0], False)
                for rd in st[1]:
                    if id(rd) not in deps:
                        deps[id(rd)] = (rd, False)
        for r in reads:
            st = self.res.setdefault(r, [None, []])
            if not o.is_dma:
                st[1] = [x for x in st[1] if x.is_dma or x.eng != eng]
            st[1].append(o)
        for w in writes:
            self.res[w] = [o, []]
        o.deps = [v for v in deps.values() if v[0] is not o]
        if o.is_dma:
            prev = dma_sem.total
            dma_sem.total += 16 * ninst
            o.token = (dma_sem, dma_sem.total)
            o.deps.append((("SEM", dma_sem, prev), True))
        self.ops[eng].append(o)
        return o

    def add_writes(self, o, names):
        for w in names:
            self.res[w] = [o, []]

    def finalize(self):
        for e in self.ENGS:
            for o in self.ops[e]:
                for d, raw in o.deps:
                    if isinstance(d, tuple):
                        continue
                    if d.is_dma:
                        continue
                    if d.eng == o.eng:
                        if SAME_ENG_SYNC and d.eng in ("act", "dve", "pool"):
                            d.need = True
                        continue
                    d.need = True
        for e in ("pool", "pe", "act", "dve"):
            s = self.eng_sem[e]
            for o in self.ops[e]:
                if not o.is_dma and o.need:
                    s.total += 1
                    o.token = (s, s.total)

    def emit(self, eng_name, eng):
        waited = {}
        for o in self.ops[eng_name]:
            need = {}
            for d, raw in o.deps:
                if isinstance(d, tuple):
                    _, s, v = d
                    if v <= 0:
                        continue
                else:
                    if (not d.is_dma) and d.eng == o.eng:
                        if not (SAME_ENG_SYNC and d.eng in ("act", "dve", "pool")):
                            continue
                    s, v = d.token
                if need.get(id(s), (s, 0))[1] < v:
                    need[id(s)] = (s, v)
            for s, v in need.values():
                if waited.get(id(s), 0) >= v:
                    continue
                eng.wait_ge(s.h, v)
                waited[id(s)] = v
            r = o.fn(eng)
            if o.is_dma:
                if not isinstance(r, (list, tuple)):
                    r = [r]
                assert len(r) == o.ninst
                for ins in r:
                    ins.then_inc(o.dma_sem.h, 16)
            elif o.need:
                r.then_inc(o.token[0].h, 1)


def build_nc(NT):
    NSLOT = NT // 4
    NTP = NT * 512
    NOWN = NSLOT * 512
    nc = bass.Bass("TRN2", target_bir_lowering=False)

    def din(name, shape, dt=F32):
        return nc.dram_tensor(name, list(shape), dt, kind="ExternalInput").ap()

    def dout(name, shape, dt=F32):
        return nc.dram_tensor(name, list(shape), dt, kind="ExternalOutput").ap()

    xsT = din("xsT", [D, NTP])
    xsmpT = din("xsmpT", [D, 128])
    ckT = din("ckT", [2, NH, 128, PAST])
    cvB = din("cvB", [2, NH, 128, 16, 128])
    cT = din("cT", [128, DC, 3])
    padflag = din("padflag", [128, 4])
    w_ada = din("w_ada", [D, 6 * D])
    b_adaT = din("b_adaT", [128, 96])
    w_adaf = din("w_adaf", [D, 2 * D])
    b_adafT = din("b_adafT", [128, 32])
    g3T = din("g3T", [128, 3, DC])
    w_in = din("w_in", [D, 5120])
    lnGB = din("lnGB", [128, 2, 1024])
    w_sT = din("w_sT", [128, 8, 128])
    bsB = din("bsB", [128, 8, 128])
    lamv = din("lamv", [128, 256])
    subgT = din("subgT", [128, 1])
    relb = din("relb", [32, 8])
    ohp = din("ohp", [32, NG])
    cmask = din("cmask", [128, 1024])
    w_out = din("w_out", [D, D])
    w_ffn_in = din("w_ffn_in", [D, 2 * DFF])
    w_ffn_out = din("w_ffn_out", [DFF, D])

    yT = dout("yT", [D, NOWN])
    ysT = dout("ysT", [D, 128])
    kT_out = dout("kT_out", [NH, 128, NOWN])
    v_out = dout("v_out", [NOWN, 1024])
    ksT_out = dout("ksT_out", [NH, 128, 128])
    vs_out = dout("vs_out", [128, 1024])
    gvs_out = dout("gvs_out", [128, 1024])

    KT = nc.dram_tensor("KTscr", [NH, 128, NTP], BF16, kind="Internal").ap()
    VB = nc.dram_tensor("VBscr", [NH, 128, NT * 4, 128], BF16, kind="Internal").ap()
    Gd2 = nc.dram_tensor("Gd2scr", [8, NG], F32, kind="Internal").ap()
    WS_in = nc.dram_tensor("WSin", [10, 128, DC, 512], BF16, kind="Internal").ap()
    WS_out = nc.dram_tensor("WSout", [4, 128, DC, 512], BF16, kind="Internal").ap()
    WS_fi = nc.dram_tensor("WSfi", [22, 128, DC, 512], BF16, kind="Internal").ap()
    WS_fo = nc.dram_tensor("WSfo", [DC, 128, FC, 128], BF16, kind="Internal").ap()

    es = ExitStack()
    with es:
        def sb(name, shape, dt=F32):
            return es.enter_context(nc.sbuf_tensor(name, list(shape), dt))

        P = Prog(nc, es)

        XH = [sb("xh%d" % i, [128, DC, 256]) for i in range(2)]
        hT = sb("hT", [128, DC, 512], BF16)
        WB = [sb("wb%d" % i, [128, DC, 512], BF16) for i in range(2)]
        BIG = sb("big", [128, 29696], BF16)
        sq = sb("sq", [128, DC, 256], BF16)
        Bf = sb("Bf", [128, NH, 1024], BF16)
        lnGB_sb = sb("lnGB_sb", [128, 2, 1024])
        bsB_sb = sb("bsB_sb", [128, 8, 128])
        wsT_sb = sb("wsT_sb", [128, 8, 128], BF16)
        gvf = sb("gvf", [128, 1, 1024])
        gvb = sb("gvb", [128, 1, 1024], BF16)
        M1 = sb("M1", [128, 96, 3])
        MF = sb("MF", [128, 32, 3])
        TAB = sb("TAB", [128, 3, 8, DC])
        cT_sb = sb("cT_sb", [128, DC * 3])
        siluT = sb("siluT", [128, DC, 3], BF16)
        badaT_sb = sb("badaT_sb", [128, 96])
        badafT_sb = sb("badafT_sb", [128, 32])
        g3T_sb = sb("g3T_sb", [128, 3, DC])
        lam_sb = sb("lam_sb", [128, 256])
        lamt = sb("lamt", [128, 8])
        subg_sb = sb("subg_sb", [128, 2])
        pad_sb = sb("pad_sb", [128, 4])
        relb_sb = sb("relb_sb", [32, 8])
        ones_bf = sb("ones_bf", [128, 128], BF16)
        rstd = sb("rstd", [128, 512])
        TMP = [sb("tmp%d" % i, [128, 512]) for i in range(4)]
        STF = [sb("stf%d" % i, [128, 512]) for i in range(2)]
        STB = [sb("stb%d" % i, [128, 1024], BF16) for i in range(2)]
        small = sb("small", [128, 16])

        def big(off, n, shape3=None):
            ap = BIG[:, off:off + n]
            if shape3 is not None:
                ap = ap.rearrange("p (a b) -> p a b", a=shape3[0], b=shape3[1])
            return ap
        Wk = big(0, 16384, (DC, 1024))
        QT = big(0, 4096, (NH, 512))
        catT = big(4096, 8192, (DC, 512))
        KBUF = [big(12288 + i * 2048, 2048) for i in range(2)]
        VBUF = [big(16384 + i * 2048, 2048, (16, 128)) for i in range(2)]
        PT = [[big(20480 + (i * 2 + t) * 512, 512) for t in range(2)] for i in range(2)]
        actT = big(0, 22528, (FC, 512))
        uT = big(22528, 4096, (NH, 512))
        ksT = big(26624, 1024, (NH, 128))
        vsb = big(27648, 2048, (2, 1024))
        xf0 = XH[0].rearrange("p a b -> p (a b)")
        xf1 = XH[1].rearrange("p a b -> p (a b)")
        cmask_f = xf0[:, 0:1024]
        ohp_sb = xf0[:, 1024:1024 + NG]
        g_sb = xf0[:, 1024 + NG:1024 + 2 * NG]
        bfr_f = [xf1[:, i * 1024:(i + 1) * 1024] for i in range(2)]
        XALL = ["xh%d_%d" % (a, b) for a in range(2) for b in range(DC)]

        PSS = [es.enter_context(nc.psum_tensor("pss%d" % i, [128, 1024], F32)) for i in range(2)]
        PS = [PSS[0][:, 0:512], PSS[0][:, 512:1024], PSS[1][:, 0:512], PSS[1][:, 512:1024]]
        PS += [es.enter_context(nc.psum_tensor("ps%d" % i, [128, 512], F32)) for i in range(4, 8)]
        PTP = [big(20480 + i * 1024, 1024) for i in range(2)]
        ACC = sq.rearrange("p a b -> p (a b)").bitcast(F32).rearrange("p (a b) -> p a b", a=4, b=512)
        ACCRES = [["sq%d" % (4 * i + q) for q in range(4)] for i in range(4)]
        ACCB = gvf.rearrange("p a b -> p (a b)").bitcast(BF16).rearrange("p (a b) -> p a b", a=4, b=512)

        def dma(q, out, in_, reads, writes, sem):
            return P.op(q, lambda e: e.dma_start(out=out, in_=in_), reads, writes, dma_sem=sem)

        def mm(out, lhsT, rhs, start, stop, reads, writes, tp=None):
            if tp is None:
                f = lambda e: e.matmul(out, lhsT, rhs, start=start, stop=stop)
            else:
                f = lambda e: e.matmul(out, lhsT, rhs, start=start, stop=stop, tile_position=tp)
            return P.op("pe", f, reads, writes)

        def act(out, in_, func, reads, writes, bias=None, scale=None):
            kw = {}
            if bias is not None:
                kw["bias"] = bias
            if scale is not None:
                kw["scale"] = scale
            return P.op("act", lambda e: e.activation(out=out, in_=in_, func=func, **kw), reads, writes)

        def tt(out, in0, in1, op, reads, writes, eng="dve"):
            return P.op(eng, lambda e: e.tensor_tensor(out=out, in0=in0, in1=in1, op=op), reads, writes)

        def ts(out, in0, s1, s2, op0, op1, reads, writes, eng="dve"):
            if op1 is None:
                f = lambda e: e.tensor_scalar(out=out, in0=in0, scalar1=s1, scalar2=None, op0=op0)
            else:
                f = lambda e: e.tensor_scalar(out=out, in0=in0, scalar1=s1, scalar2=s2, op0=op0, op1=op1)
            return P.op(eng, f, reads, writes)

        def stt(out, in0, scalar, in1, op0, op1, reads, writes, eng="dve"):
            return P.op(eng, lambda e: e.scalar_tensor_tensor(out=out, in0=in0, scalar=scalar, in1=in1,
                                                              op0=op0, op1=op1), reads, writes)

        def cp(out, in_, reads, writes, eng="dve"):
            return P.op(eng, lambda e: e.tensor_copy(out=out, in_=in_), reads, writes)

        def recip(out, in_, reads, writes):
            return P.op("dve", lambda e: e.reciprocal(out=out, in_=in_), reads, writes)

        sem_misc = [P.new_sem() for _ in range(4)]
        mi = [0]

        def msem():
            mi[0] += 1
            return sem_misc[mi[0] % 4]

        psrr = [0]

        def rstd_from_ps(ps_ap, n, inv_n, tag):
            ts(rstd[:, 0:n], ps_ap, inv_n, EPS, ALU.mult, ALU.add, [tag], ["rstd"])
            act(rstd[:, 0:n], rstd[:, 0:n], AF.Sqrt, ["rstd"], ["rstd"])
            recip(rstd[:, 0:n], rstd[:, 0:n], ["rstd"], ["rstd"])

        tmprr = [0]

        def norm_mod(xsrc, n, row, kG, kSH, out_fn, psb):
            for c0 in range(0, n, 256):
                cn = min(256, n - c0)
                for dc in range(DC):
                    xa, xr = xsrc(dc)
                    act(sq[:, dc, 0:cn], xa[:, c0:c0 + cn], AF.Square, [xr], ["sq%d" % dc])
                for dc in range(DC):
                    mm(PS[psb][:, c0:c0 + cn], ones_bf[:, :], sq[:, dc, 0:cn], dc == 0, dc == DC - 1,
                       ["sq%d" % dc, "ones"], ["ps%d" % psb])
            rstd_from_ps(PS[psb][:, 0:n], n, 1.0 / D, "ps%d" % psb)
            for dc in range(DC):
                xa, xr = xsrc(dc)
                t = tmprr[0] % 4
                tmprr[0] += 1
                tt(TMP[t][:, 0:n], xa, rstd[:, 0:n], ALU.mult, [xr, "rstd"], ["tmp%d" % t])
                oa, orr = out_fn(dc)
                act(oa, TMP[t][:, 0:n], AF.Identity, ["tmp%d" % t, "TAB"], [orr],
                    bias=TAB[:, row, kSH, dc:dc + 1], scale=TAB[:, row, kG, dc:dc + 1])

        s0 = P.new_sem()
        for (dst, src, nm) in [(cT_sb[:, :], cT.rearrange("p a b -> p (a b)"), "cT"),
                               (badaT_sb[:, :], b_adaT, "bada"), (badafT_sb[:, :], b_adafT, "badaf"),
                               (g3T_sb[:, :, :], g3T, "g3T"), (lam_sb[:, :], lamv, "lamv"),
                               (subg_sb[:, 0:1], subgT, "subg"), (pad_sb[:, :], padflag, "pad"),
                               (relb_sb[:, :], relb, "relb"), (ohp_sb[0:32, :], ohp, "XALL"),
                               (lnGB_sb[:, :, :], lnGB, "lnGB"), (bsB_sb[:, :, :], bsB, "bsB"),
                               (cmask_f, cmask, "XALL")]:
            dma("sp", dst, src, [], (XALL if nm == "XALL" else [nm]), s0)
        s0b = P.new_sem()
        dma("pool", wsT_sb[:, :, :], w_sT, [], ["wsT"], s0b)
        P.op("dve", lambda e: e.memset(ones_bf[:, :], 1.0), [], ["ones"])
        P.op("dve", lambda e: e.memset(wsT_sb[64:128, :, 0:64], 0.0), [], ["wsT"])
        act(siluT.rearrange("p a b -> p (a b)"), cT_sb[:, :], AF.Silu, ["cT"], ["siluT"])
        tt(TMP[0][:, 0:64], lam_sb[:, 0:64], lam_sb[:, 64:128], ALU.mult, ["lamv"], ["tmp0"])
        P.op("dve", lambda e: e.reduce_sum(out=lamt[:, 0:1], in_=TMP[0][:, 0:64], axis=AX.X), ["tmp0"], ["lamt"])
        tt(TMP[1][:, 0:64], lam_sb[:, 128:192], lam_sb[:, 192:256], ALU.mult, ["lamv"], ["tmp1"])
        P.op("dve", lambda e: e.reduce_sum(out=lamt[:, 1:2], in_=TMP[1][:, 0:64], axis=AX.X), ["tmp1"], ["lamt"])
        act(lamt[:, 2:4], lamt[:, 0:2], AF.Exp, ["lamt"], ["lamt2"])
        tt(lamt[:, 4:5], lamt[:, 3:4], lamt[:, 2:3], ALU.subtract, ["lamt2"], ["lamt3"])
        ts(lamt[:, 5:6], lamt[:, 4:5], -LAM_INIT, None, ALU.add, None, ["lamt3"], ["neglam"])
        neglam = lamt[:, 5:6]
        ts(subg_sb[:, 1:2], subg_sb[:, 0:1], 1.0 - LAM_INIT, None, ALU.mult, None, ["subg"], ["subg2"])
        subg2 = subg_sb[:, 1:2]

        for pi, (n0, nn) in enumerate([(0, 512), (512, 512), (1024, NG - 1024)]):
            mm(PS[pi][0:8, 0:nn], relb_sb[0:32, 0:8], ohp_sb[0:32, n0:n0 + nn], True, True,
               ["relb"] + XALL, ["ps%d" % pi])
            cp(g_sb[0:8, n0:n0 + nn], PS[pi][0:8, 0:nn], ["ps%d" % pi], ["g_sb"])
        sg = P.new_sem()
        dma("sp", Gd2[:, :], g_sb[0:8, :], ["g_sb"], ["Gd2"], sg)
        sbf = [P.new_sem(), P.new_sem()]
        for h in range(NH):
            b = h % 2
            src = bass.AP(Gd2.tensor, h * NG, [[1, 128], [1, 1024]])
            dma("sp", bfr_f[b], src, ["Gd2"], ["bfr%d" % b], sbf[b])
            rev = bass.AP(bfr_f[b].tensor, bfr_f[b].offset + 1023, [list(bfr_f[b].ap[0]), [-1, 1024]])
            tt(Bf[:, h, :], rev, cmask_f, ALU.add, ["bfr%d" % b] + XALL, ["Bf"])

        wsem = [P.new_sem(), P.new_sem()]
        wcnt = [0]

        def wload(src_ap, view=None, res=()):
            i = wcnt[0] % 2
            wcnt[0] += 1
            dst = WB[i][:, :, :] if view is None else view(WB[i])
            dma("pool", dst, src_ap, list(res), ["wb%d" % i], wsem[i])
            return i

        w_ada_v = w_ada.rearrange("(dc p) e -> p dc e", p=128)
        w_adaf_v = w_adaf.rearrange("(dc p) e -> p dc e", p=128)
        for (wv, npan, psb, Mdst, bsrc, nec) in [(w_ada_v, 24, 6, M1, badaT_sb, 96),
                                                 (w_adaf_v, 8, 7, MF, badafT_sb, 32)]:
            for pn in range(npan):
                i = wload(wv[:, :, pn * 512:(pn + 1) * 512])
                for cb in range(4):
                    ec = pn * 4 + cb
                    for dc in range(DC):
                        mm(PS[psb][:, ec * 3:ec * 3 + 3], WB[i][:, dc, cb * 128:(cb + 1) * 128],
                           siluT[:, dc, :], dc == 0, dc == DC - 1, ["wb%d" % i, "siluT"], ["ps%d" % psb])
            pv = PS[psb][:, 0:nec * 3].rearrange("p (a b) -> p a b", a=nec, b=3)
            for r in range(3):
                tt(Mdst[:, :, r], pv[:, :, r], bsrc[:, :], ALU.add, ["ps%d" % psb, "bada", "badaf"], ["Mada"])
        for r in range(3):
            for (kG, gsel, sc_ap) in [(0, 0, M1[:, 16:32, r]), (3, 1, M1[:, 64:80, r]), (6, 2, MF[:, 16:32, r])]:
                stt(TAB[:, r, kG, :], sc_ap, 1.0, g3T_sb[:, gsel, :], ALU.add, ALU.mult, ["Mada", "g3T"], ["TAB"])
            for (k, src_ap) in [(1, M1[:, 0:16, r]), (2, M1[:, 32:48, r]), (4, M1[:, 48:64, r]),
                                (5, M1[:, 80:96, r]), (7, MF[:, 0:16, r])]:
                cp(TAB[:, r, k, :], src_ap, ["Mada"], ["TAB"])

        w_in_v = w_in.rearrange("(dc p) e -> p dc e", p=128)
        swk = P.new_sem()
        for half in range(2):
            dma("pool", Wk[:, :, half * 512:(half + 1) * 512], w_in_v[:, :, 3072 + half * 512:3072 + (half + 1) * 512],
                ["Bf"], ["Wk"], swk)
            dma("pool", WB[half][:, :, :], w_in_v[:, :, 4096 + half * 512:4096 + (half + 1) * 512],
                ["Bf"], ["wb%d" % half], swk)
        xsT_v = xsT.rearrange("(dc p) t -> p dc t", p=128)
        xsem = [P.new_sem(), P.new_sem()]
        stsem = [P.new_sem() for _ in range(4)]
        osem = [P.new_sem() for _ in range(4)]
        stc = [0]
        oc = [0]

        def out_dma(dst, src, reads):
            s = osem[oc[0] % 4]
            oc[0] += 1
            o = dma("sp", dst, src, reads, [], s)
            P.out_ops.append(o)
            return o

        def load_x_halves(col0):
            for hf in range(2):
                dma("sp", XH[hf][:, :, :], xsT_v[:, :, col0 + hf * 256:col0 + (hf + 1) * 256], [],
                    ["xh%d_%d" % (hf, dc) for dc in range(DC)] + ["bfr0", "bfr1", "g_sb"], xsem[hf])

        psA = [0]

        def nextps(lst):
            psA[0] += 1
            return lst[psA[0] % len(lst)]

        w_out_v0 = w_out.rearrange("(dc p) e -> p dc e", p=128)
        w_fi_v0 = w_ffn_in.rearrange("(dc p) e -> p dc e", p=128)
        w_fo_v0 = w_ffn_out.rearrange("(fc p) e -> p fc e", p=128)
        csem = [P.new_sem(), P.new_sem()]
        cc = [0]

        def conv(dst, src, res):
            k = cc[0] % 2
            cc[0] += 1
            dma("pool", dst, src, [], [res], csem[k])
        if STOP >= 3:
            for pn in range(6):
                conv(WS_in[pn], w_in_v[:, :, pn * 512:(pn + 1) * 512], "ws_in%d" % pn)
            for pn in range(4):
                conv(WS_out[pn], w_out_v0[:, :, pn * 512:(pn + 1) * 512], "ws_out%d" % pn)
            for pn in range(22):
                conv(WS_fi[pn][:, :, 0:256], w_fi_v0[:, :, pn * 256:(pn + 1) * 256], "ws_fi%d" % pn)
                conv(WS_fi[pn][:, :, 256:512], w_fi_v0[:, :, DFF + pn * 256:DFF + (pn + 1) * 256], "ws_fi%d" % pn)
            for oc_ in range(DC):
                conv(WS_fo[oc_], w_fo_v0[:, :, oc_ * 128:(oc_ + 1) * 128], "ws_fo%d" % oc_)
            for pn in range(6, 10):
                conv(WS_in[pn], w_in_v[:, :, pn * 512:(pn + 1) * 512], "ws_in%d" % pn)

        hbuf = [hT, big(16384, 8192, (DC, 512))]
        hname = ["hT", "hU"]

        def norm_a_sq(hf):
            for dc in range(DC):
                act(sq[:, dc, 0:256], XH[hf][:, dc, :], AF.Square, ["xh%d_%d" % (hf, dc)], ["sq%d" % dc])

        def norm_a_rest(hf, hb):
            for dc in range(DC):
                mm(PS[7][:, 0:256], ones_bf[:, :], sq[:, dc, 0:256], dc == 0, dc == DC - 1,
                   ["sq%d" % dc, "ones"], ["ps7"])
            rstd_from_ps(PS[7][:, 0:256], 256, 1.0 / D, "ps7")
            for dc in range(DC):
                t = tmprr[0] % 4
                tmprr[0] += 1
                tt(TMP[t][:, 0:256], XH[hf][:, dc, :], rstd[:, 0:256], ALU.mult, ["xh%d_%d" % (hf, dc), "rstd"],
                   ["tmp%d" % t])
                act(hbuf[hb][:, dc, hf * 256:(hf + 1) * 256], TMP[t][:, 0:256], AF.Identity, ["tmp%d" % t, "TAB"],
                    ["%s%d_%d" % (hname[hb], hf, dc)], bias=TAB[:, 0, 1, dc:dc + 1], scale=TAB[:, 0, 0, dc:dc + 1])

        def load_x_half(col0, hf):
            dma("sp", XH[hf][:, :, :], xsT_v[:, :, col0 + hf * 256:col0 + (hf + 1) * 256], [],
                ["xh%d_%d" % (hf, dc) for dc in range(DC)] + ["bfr0", "bfr1", "g_sb"], xsem[hf])

        if NT > 0 and STOP >= 2:
            for hf in range(2):
                load_x_half(0, hf)
            for hf in range(2):
                norm_a_sq(hf)
                norm_a_rest(hf, 0)
                if NT > 1:
                    load_x_half(512, hf)
        for u in range(NT if STOP >= 2 else 0):
            own = (u % 4 == 3)
            slot = u // 4
            hb = u % 2
            hcur = hbuf[hb]
            hres = lambda dc, hb=hb: ["%s0_%d" % (hname[hb], dc), "%s1_%d" % (hname[hb], dc)]
            nxt = (u + 1 < NT)
            if nxt:
                norm_a_sq(0)
            for h in range(NH):
                pb = nextps([0, 2, 4, 5, 6])
                for dc in range(DC):
                    mm(PS[pb][:, :], Wk[:, dc, h * 128:(h + 1) * 128], hcur[:, dc, :], dc == 0, dc == DC - 1,
                       ["Wk"] + hres(dc), ["ps%d" % pb])
                sbi = stc[0] % 2
                stc[0] += 1
                if own:
                    fi = stc[0] % 2
                    cp(STF[fi][:, :], PS[pb][:, :], ["ps%d" % pb], ["stf%d" % fi])
                    out_dma(kT_out[h, :, slot * 512:(slot + 1) * 512], STF[fi][:, :], ["stf%d" % fi])
                    act(STB[sbi][:, 0:512], STF[fi][:, :], AF.Copy, ["stf%d" % fi], ["stb%d" % sbi])
                else:
                    act(STB[sbi][:, 0:512], PS[pb][:, :], AF.Copy, ["ps%d" % pb], ["stb%d" % sbi])
                dma("sp", KT[h, :, u * 512:(u + 1) * 512], STB[sbi][:, 0:512], ["stb%d" % sbi], ["KT%d_%d" % (u, h)],
                    stsem[sbi])
                if h == 1 and nxt:
                    norm_a_rest(0, 1 - hb)
                    if u + 2 < NT:
                        load_x_half((u + 2) * 512, 0)
                if h == 3 and nxt:
                    norm_a_sq(1)
            for s in range(4):
                sbi = stc[0] % 2
                stc[0] += 1
                for cg in range(2):
                    pb = nextps([0, 2, 4, 5, 6])
                    for dc in range(DC):
                        mm(PS[pb][:, :], hcur[:, dc, s * 128:(s + 1) * 128], WB[cg][:, dc, :],
                           dc == 0, dc == DC - 1, ["wb%d" % cg] + hres(dc), ["ps%d" % pb])
                    if own:
                        fi = (stc[0] + cg) % 2
                        cp(STF[fi][:, :], PS[pb][:, :], ["ps%d" % pb], ["stf%d" % fi])
                        out_dma(v_out[slot * 512 + s * 128:slot * 512 + (s + 1) * 128, cg * 512:(cg + 1) * 512],
                                STF[fi][:, :], ["stf%d" % fi])
                        act(STB[sbi][:, cg * 512:(cg + 1) * 512], STF[fi][:, :], AF.Copy, ["stf%d" % fi], ["stb%d" % sbi])
                    else:
                        act(STB[sbi][:, cg * 512:(cg + 1) * 512], PS[pb][:, :], AF.Copy, ["ps%d" % pb], ["stb%d" % sbi])
                dma("sp", VB[:, :, u * 4 + s, :].rearrange("h p d -> p h d"),
                    STB[sbi][:, :].rearrange("p (h d) -> p h d", h=NH, d=128), ["stb%d" % sbi], ["VB%d_%d" % (u, s)],
                    stsem[2 + sbi])
                if s == 0 and nxt:
                    norm_a_rest(1, 1 - hb)
                    if u + 2 < NT:
                        load_x_half((u + 2) * 512, 1)

        kvsem = [P.new_sem(), P.new_sem()]
        kvsem_s = [P.new_sem(), P.new_sem()]
        kvc = [0]
        w_out_v = w_out.rearrange("(dc p) e -> p dc e", p=128)
        w_fi_v = w_ffn_in.rearrange("(dc p) e -> p dc e", p=128)
        w_fo_v = w_ffn_out.rearrange("(fc p) e -> p fc e", p=128)

        def attn_run(heads):
            flat = [(hi, ci) for hi, hd in enumerate(heads) for ci in range(len(hd["chunks"]))]
            loaded = set()

            def load(k):
                if k < len(flat) and k not in loaded:
                    loaded.add(k)
                    hi, ci = flat[k]
                    ld = heads[hi]["chunks"][ci][0]
                    if ld is not None:
                        ld()
            load(0)
            fk = 0
            pending = []
            for hi, hd in enumerate(heads):
                q_ap, nq, qres = hd["q_ap"], hd["nq"], hd["qres"]
                blks = []
                for ci, (ld, blocks) in enumerate(hd["chunks"]):
                    for bj, blk in enumerate(blocks):
                        blks.append((blk, fk + ci if bj == 0 else None))
                nblk = len(blks)

                def emit_qk(b):
                    (kT_ap, v_ap, nk, bias_ap, flag_ap, kres, vres), pk = blks[b]
                    if pk is not None:
                        load(pk)
                    sp_ = b % 2
                    mm(PSS[sp_][0:nk, 0:nq], kT_ap[0:64, :], q_ap[0:64, :], True, True, [kres, qres],
                       ["pss%d" % sp_], tp=(0, 0))
                    mm(PSS[sp_][0:nk, 512:512 + nq], kT_ap[64:128, :], q_ap[64:128, :], True, True, [kres, qres],
                       ["pss%d" % sp_], tp=(64, 0))

                obase = 4
                accp = hi % 2
                use_pool = (nq == 512)

                def emit_rest(b):
                    (kT_ap, v_ap, nk, bias_ap, flag_ap, kres, vres), pk = blks[b]
                    if pk is not None:
                        load(pk + 1)
                    sp_ = b % 2
                    first = (b == 0)
                    last = (b == nblk - 1)
                    pres = "ptp%d" % sp_
                    if nq == 512:
                        src = PSS[sp_][0:nk, :]
                        dst = PTP[sp_][0:nk, :]
                    else:
                        src = PSS[sp_][0:nk, :].rearrange("p (t n) -> p t n", t=2, n=512)[:, :, 0:nq]
                        dst = PTP[sp_][0:nk, :].rearrange("p (t n) -> p t n", t=2, n=512)[:, :, 0:nq]
                    rd = ["pss%d" % sp_]
                    if bias_ap is not None:
                        for t in range(2):
                            tmi = 2 * sp_ + t
                            tt(TMP[tmi][0:nk, 0:nq], PSS[sp_][0:nk, t * 512:t * 512 + nq], bias_ap, ALU.add,
                               ["pss%d" % sp_, "Bf"], ["tmp%d" % tmi])
                            kw = {}
                            if flag_ap is not None:
                                kw["bias"] = flag_ap[0:nk, :]
                            act(PTP[sp_][0:nk, t * 512:t * 512 + nq], TMP[tmi][0:nk, 0:nq], AF.Exp,
                                ["tmp%d" % tmi, "pad"], [pres], **kw)
                    elif flag_ap is not None:
                        act(dst, src, AF.Exp, rd + ["pad"], [pres], bias=flag_ap[0:nk, :])
                    else:
                        act(dst, src, AF.Exp, rd, [pres])
                    for t in range(2):
                        pt = PTP[sp_][0:nk, t * 512:t * 512 + nq]
                        mm(PS[obase + t][:, 0:nq], v_ap, pt, first, last, [vres, pres], ["ps%d" % (obase + t)])
                    for t in range(2):
                        pt = PTP[sp_][0:nk, t * 512:t * 512 + nq]
                        mm(PS[6 + t][:, 0:nq], ones_bf[0:nk, :], pt, first, last, ["ones", pres], ["ps%d" % (6 + t)])

                def fin_stage1(nq=nq):
                    for q in range(4):
                        cp(TMP[q][:, 0:nq], PS[4 + q][:, 0:nq], ["ps%d" % (4 + q)], ["tmp%d" % q])

                def fin_stage2(nq=nq):
                    recip(TMP[2][:, 0:nq], TMP[2][:, 0:nq], ["tmp2"], ["tmp2"])
                    tt(TMP[0][:, 0:nq], TMP[0][:, 0:nq], TMP[2][:, 0:nq], ALU.mult, ["tmp0", "tmp2"], ["tmp0"])
                    recip(TMP[3][:, 0:nq], TMP[3][:, 0:nq], ["tmp3"], ["tmp3"])
                    tt(TMP[1][:, 0:nq], TMP[1][:, 0:nq], TMP[3][:, 0:nq], ALU.mult, ["tmp1", "tmp3"], ["tmp1"])
                    stt(TMP[0][:, 0:nq], TMP[1][:, 0:nq], neglam, TMP[0][:, 0:nq], ALU.mult, ALU.add,
                        ["tmp0", "tmp1", "neglam"], ["tmp0"])

                def fin_stage2b(nq=nq):
                    act(ACCB[:, 1, 0:nq], TMP[0][:, 0:nq], AF.Square, ["tmp0"], ["accb1"])

                def fin_stage3(out_ap=hd["out_ap"], out_res=hd["out_res"], nq=nq, last_head=(hi == len(heads) - 1)):
                    pb = 7 if last_head else 0
                    mm(PS[pb][:, 0:nq], ones_bf[:, :], ACCB[:, 1, 0:nq], True, True, ["ones", "accb1"], ["ps%d" % pb])
                    rstd_from_ps(PS[pb][:, 0:nq], nq, 1.0 / 128, "ps%d" % pb)
                    tt(TMP[0][:, 0:nq], TMP[0][:, 0:nq], rstd[:, 0:nq], ALU.mult, ["tmp0", "rstd"], ["tmp0"])
                    act(out_ap, TMP[0][:, 0:nq], AF.Identity, ["tmp0", "subg2"], [out_res], scale=subg2)

                emit_qk(0)
                for b in range(nblk):
                    if b + 1 < nblk:
                        emit_qk(b + 1)
                    emit_rest(b)
                    if pending and b == 1:
                        pending[0]()
                    if pending and b == 6:
                        pending[1]()
                    if pending and b == 10:
                        pending[2]()
                        del pending[:]
                fk += len(hd["chunks"])
                fin_stage1()
                pending.extend([fin_stage2, fin_stage2b, fin_stage3])
            if pending:
                pending[0]()
                pending[1]()
                pending[2]()
                del pending[:]

        def do_slot(slot, sample):
            NTOK = 128 if sample else 512
            if sample:
                subt = [(0, 64, 1), (64, 64, 2)]
                colr = [(0, 64, 1), (64, 64, 2)]
            else:
                subt = [(i * 128, 128, 0) for i in range(4)]
                colr = [(0, 512, 0)]
            if sample:
                dma("sp", XH[0][:, :, 0:128], xsmpT.rearrange("(dc p) t -> p dc t", p=128), [],
                    ["xh0_%d" % dc for dc in range(DC)], xsem[0])
            else:
                load_x_halves((4 * slot + 3) * 512)

            def xpieces(c0, n):
                out = []
                for hf in range(2):
                    a = max(c0, hf * 256)
                    b = min(c0 + n, (hf + 1) * 256)
                    if b > a:
                        out.append((hf, a - hf * 256, b - a, a))
                return out

            def do_norm(kG, kSH, fp32_out=None):
                for (c0, n, row) in colr:
                    for (hf, lc, ln_, gc) in xpieces(c0, n):
                        if fp32_out is None:
                            ofn = lambda dc, hf=hf, gc=gc, ln_=ln_: (hT[:, dc, gc:gc + ln_], "hT%d_%d" % (hf, dc))
                        else:
                            ofn = fp32_out(hf, lc, ln_, gc)
                        norm_mod(lambda dc, hf=hf, lc=lc, ln_=ln_: (XH[hf][:, dc, lc:lc + ln_], "xh%d_%d" % (hf, dc)),
                                 ln_, row, kG, kSH, ofn, 7)

            do_norm(0, 1)
            hres = (lambda dc: ["hT0_%d" % dc]) if sample else (lambda dc: ["hT0_%d" % dc, "hT1_%d" % dc])

            def gv_ln_gate(si, c0, n, gb):
                g = gvf[0:n, gb, :]
                gr = "gvf%d" % gb
                gbr = "gvb%d" % gb
                P.op("dve", lambda e: e.reduce_sum(out=small[0:n, 0:1], in_=g, axis=AX.X), [gr], ["small"])
                ts(small[0:n, 1:2], small[0:n, 0:1], -1.0 / 1024, None, ALU.mult, None, ["small"], ["small"])
                ts(g, g, small[0:n, 1:2], None, ALU.add, None, [gr, "small"], [gr])
                tt(gvb[0:n, gb, :], g, g, ALU.mult, [gr], [gbr])
                P.op("dve", lambda e: e.reduce_sum(out=small[0:n, 2:3], in_=gvb[0:n, gb, :], axis=AX.X),
                     [gbr], ["small"])
                ts(small[0:n, 3:4], small[0:n, 2:3], 1.0 / 1024, EPS, ALU.mult, ALU.add, ["small"], ["small"])
                act(small[0:n, 3:4], small[0:n, 3:4], AF.Sqrt, ["small"], ["small"])
                recip(small[0:n, 4:5], small[0:n, 3:4], ["small"], ["small"])
                stt(g, g, small[0:n, 4:5], lnGB_sb[0:n, 0, :], ALU.mult, ALU.mult, [gr, "small", "lnGB"], [gr])
                tt(g, g, lnGB_sb[0:n, 1, :], ALU.add, [gr, "lnGB"], [gr])
                cp(gvb[0:n, gb, :], g, [gr], [gbr])
                if sample:
                    out_dma(gvs_out[c0:c0 + n, :], g, [gr])
                for g0 in (0, 4):
                    pb = nextps([0, 2, 4, 5, 6])
                    for gg in range(4):
                        gi = g0 + gg
                        mm(PS[pb][:, gg * n:(gg + 1) * n], gvb[0:n, gb, gi * 128:(gi + 1) * 128],
                           wsT_sb[0:n, gi, 0:n], True, True, [gbr, "wsT"], ["ps%d" % pb])
                    t = tmprr[0] % 4
                    tmprr[0] += 1
                    pv3 = PS[pb][:, 0:4 * n].rearrange("p (a b) -> p a b", a=4, b=n)
                    tv3 = TMP[t][:, 0:4 * n].rearrange("p (a b) -> p a b", a=4, b=n)
                    tt(tv3, pv3, bsB_sb[:, g0:g0 + 4, 0:n], ALU.add, ["ps%d" % pb, "bsB"], ["tmp%d" % t])
                    tt(catT[:, g0:g0 + 4, c0:c0 + n], tv3, uT[:, g0:g0 + 4, c0:c0 + n], ALU.mult,
                       ["tmp%d" % t] + ["uT%d" % q for q in range(g0, g0 + 4)],
                       ["cat%d" % q for q in range(g0, g0 + 4)])

            npan = 10 if sample else 6
            for pn in range(npan):
                i = wload(WS_in[pn], res=["ws_in%d" % pn])
                wr = "wb%d" % i
                if pn < 2:
                    for cb in range(4):
                        ec = pn * 4 + cb
                        pb = nextps([0, 2, 4, 5, 6])
                        for dc in range(DC):
                            mm(PS[pb][:, 0:NTOK], WB[i][:, dc, cb * 128:(cb + 1) * 128], hT[:, dc, 0:NTOK],
                               dc == 0, dc == DC - 1, [wr] + hres(dc), ["ps%d" % pb])
                        act(uT[:, ec, 0:NTOK], PS[pb][:, 0:NTOK], AF.Gelu_apprx_tanh, ["ps%d" % pb], ["uT%d" % ec])
                elif pn == 2:
                    i2 = i
                    continue
                elif pn == 3:
                    i3 = i
                    for si, (c0, n, row) in enumerate(subt):
                        gb = 0
                        for cg, iw in enumerate((i2, i3)):
                            pb = nextps([0, 2, 4, 5, 6])
                            for dc in range(DC):
                                mm(PS[pb][0:n, :], hT[:, dc, c0:c0 + n], WB[iw][:, dc, :], dc == 0, dc == DC - 1,
                                   ["wb%d" % iw] + hres(dc), ["ps%d" % pb])
                            act(gvf[0:n, gb, cg * 512:(cg + 1) * 512], PS[pb][0:n, :], AF.Gelu_apprx_tanh,
                                ["ps%d" % pb], ["gvf%d" % gb])
                        gv_ln_gate(si, c0, n, gb)
                elif pn < 6:
                    for cb in range(4):
                        hh = (pn - 4) * 4 + cb
                        pb = nextps([0, 2, 4, 5, 6])
                        for dc in range(DC):
                            mm(PS[pb][:, 0:NTOK], WB[i][:, dc, cb * 128:(cb + 1) * 128], hT[:, dc, 0:NTOK],
                               dc == 0, dc == DC - 1, [wr] + hres(dc), ["ps%d" % pb])
                        act(QT[:, hh, 0:NTOK], PS[pb][:, 0:NTOK], AF.Copy, ["ps%d" % pb], ["QT%d" % hh], scale=0.125)
                elif pn < 8:
                    for cb in range(4):
                        hh = (pn - 6) * 4 + cb
                        pb = nextps([0, 2, 4, 5, 6])
                        for dc in range(DC):
                            mm(PS[pb][:, 0:NTOK], WB[i][:, dc, cb * 128:(cb + 1) * 128], hT[:, dc, 0:NTOK],
                               dc == 0, dc == DC - 1, [wr] + hres(dc), ["ps%d" % pb])
                        fi = hh % 2
                        cp(STF[fi][:, 0:NTOK], PS[pb][:, 0:NTOK], ["ps%d" % pb], ["stf%d" % fi])
                        out_dma(ksT_out[hh, :, :], STF[fi][:, 0:NTOK], ["stf%d" % fi])
                        act(ksT[:, hh, 0:NTOK], STF[fi][:, 0:NTOK], AF.Copy, ["stf%d" % fi], ["ksT%d" % hh])
                else:
                    for si, (c0, n, row) in enumerate(subt):
                        pb = nextps([0, 2, 4, 5, 6])
                        for dc in range(DC):
                            mm(PS[pb][0:n, :], hT[:, dc, c0:c0 + n], WB[i][:, dc, :], dc == 0, dc == DC - 1,
                               [wr] + hres(dc), ["ps%d" % pb])
                        fi = (si + pn) % 2
                        cp(STF[fi][0:n, :], PS[pb][0:n, :], ["ps%d" % pb], ["stf%d" % fi])
                        out_dma(vs_out[c0:c0 + n, (pn - 8) * 512:(pn - 7) * 512], STF[fi][0:n, :], ["stf%d" % fi])
                        act(vsb[0:n, si, (pn - 8) * 512:(pn - 7) * 512], STF[fi][0:n, :], AF.Copy, ["stf%d" % fi],
                            ["vsb%d" % si])

            heads = []
            for h in range(NH):
                if not sample:
                    chunks = []
                    for c in range(slot + 1):
                        bsel = kvc[0] % 2
                        kvc[0] += 1

                        def loader(c=c, bsel=bsel, h=h):
                            dma("sp", KBUF[bsel], KT[h, :, c * 2048:(c + 1) * 2048],
                                ["KT%d_%d" % (uu, h) for uu in range(4 * c, 4 * c + 4)], ["kbuf%d" % bsel], kvsem[bsel])
                            dma("sp", VBUF[bsel], VB[h, :, c * 16:(c + 1) * 16, :],
                                ["VB%d_%d" % (uu, ss) for uu in range(4 * c, 4 * c + 4) for ss in range(4)],
                                ["vbuf%d" % bsel], kvsem[bsel])
                        blocks = []
                        for kb in range(16):
                            t512 = c * 4 + kb // 4
                            bias_ap = None
                            if c == slot and kb >= 11:
                                r = kb - 12
                                off = 384 - 128 * r
                                bias_ap = Bf[:, h, off:off + 512]
                            flag_ap = pad_sb[:, t512:t512 + 1] if t512 < 3 else None
                            blocks.append((KBUF[bsel][:, kb * 128:(kb + 1) * 128], VBUF[bsel][:, kb, :], 128, bias_ap,
                                           flag_ap, "kbuf%d" % bsel, "vbuf%d" % bsel))
                        chunks.append((loader, blocks))
                    heads.append(dict(q_ap=QT[:, h, :], nq=512, chunks=chunks, out_ap=catT[:, 8 + h, :],
                                      out_res="cat%d" % (8 + h), qres="QT%d" % h))
                else:
                    for st in range(2):
                        bsel = kvc[0] % 2
                        kvc[0] += 1

                        def loader(st=st, bsel=bsel, h=h):
                            dma("pool", KBUF[bsel], ckT[st, h, :, :], [], ["kbuf%d" % bsel], kvsem_s[bsel])
                            dma("pool", VBUF[bsel], cvB[st, h, :, :, :], [], ["vbuf%d" % bsel], kvsem_s[bsel])
                        blocks = []
                        for kb in range(16):
                            bias_ap = Bf[:, h, 512:576] if kb == 15 else None
                            blocks.append((KBUF[bsel][:, kb * 128:(kb + 1) * 128], VBUF[bsel][:, kb, :], 128, bias_ap,
                                           None, "kbuf%d" % bsel, "vbuf%d" % bsel))
                        blocks.append((ksT[:, h, st * 64:(st + 1) * 64], vsb[0:64, st, h * 128:(h + 1) * 128], 64,
                                       Bf[0:64, h, 384:448], None, "ksT%d" % h, "vsb%d" % st))
                        heads.append(dict(q_ap=QT[:, h, st * 64:(st + 1) * 64], nq=64, chunks=[(loader, blocks)],
                                          out_ap=catT[:, 8 + h, st * 64:(st + 1) * 64], out_res="cat%d" % (8 + h),
                                          qres="QT%d" % h))
            attn_run(heads)

            for pn in range(4):
                i = wload(WS_out[pn], res=["ws_out%d" % pn])
                for cb in range(4):
                    oc_ = pn * 4 + cb
                    pb = nextps([0, 2, 4, 5, 6])
                    for ec in range(DC):
                        mm(PS[pb][:, 0:NTOK], WB[i][:, ec, cb * 128:(cb + 1) * 128], catT[:, ec, 0:NTOK],
                           ec == 0, ec == DC - 1, ["wb%d" % i, "cat%d" % ec], ["ps%d" % pb])
                    for (c0, n, row) in colr:
                        for (hf, lc, ln_, gc) in xpieces(c0, n):
                            stt(XH[hf][:, oc_, lc:lc + ln_], PS[pb][:, gc:gc + ln_], TAB[:, row, 2, oc_:oc_ + 1],
                                XH[hf][:, oc_, lc:lc + ln_], ALU.mult, ALU.add,
                                ["ps%d" % pb, "TAB", "xh%d_%d" % (hf, oc_)], ["xh%d_%d" % (hf, oc_)])

            do_norm(3, 4)
            for pn in range(22):
                i = wload(WS_fi[pn], res=["ws_fi%d" % pn])
                for fl in range(2):
                    fc = pn * 2 + fl
                    pg = nextps([0, 2, 4, 5, 6])
                    for dc in range(DC):
                        mm(PS[pg][:, 0:NTOK], WB[i][:, dc, fl * 128:(fl + 1) * 128], hT[:, dc, 0:NTOK],
                           dc == 0, dc == DC - 1, ["wb%d" % i] + hres(dc), ["ps%d" % pg])
                    pu = nextps([0, 2, 4, 5, 6])
                    for dc in range(DC):
                        mm(PS[pu][:, 0:NTOK], WB[i][:, dc, 256 + fl * 128:256 + (fl + 1) * 128], hT[:, dc, 0:NTOK],
                           dc == 0, dc == DC - 1, ["wb%d" % i] + hres(dc), ["ps%d" % pu])
                    t = tmprr[0] % 4
                    tmprr[0] += 1
                    act(TMP[t][:, 0:NTOK], PS[pg][:, 0:NTOK], AF.Silu, ["ps%d" % pg], ["tmp%d" % t])
                    tt(actT[:, fc, 0:NTOK], TMP[t][:, 0:NTOK], PS[pu][:, 0:NTOK], ALU.mult,
                       ["tmp%d" % t, "ps%d" % pu], ["actT%d" % fc])
            for oc_ in range(DC):
                i = wcnt[0] % 2
                wcnt[0] += 1
                wview = WB[i].rearrange("p a b -> p (a b)")[:, 0:FC * 128].rearrange("p (f c) -> p f c", f=FC, c=128)
                dma("pool", wview, WS_fo[oc_], ["ws_fo%d" % oc_], ["wb%d" % i], wsem[i])
                pb = nextps([0, 2, 4, 5, 6])
                for fc in range(FC):
                    lastmm = mm(PS[pb][:, 0:NTOK], wview[:, fc, :], actT[:, fc, 0:NTOK], fc == 0, fc == FC - 1,
                                ["wb%d" % i, "actT%d" % fc], ["ps%d" % pb])
                if oc_ == DC - 1:
                    P.add_writes(lastmm, ["kbuf0", "kbuf1", "vbuf0", "vbuf1"])
                for (c0, n, row) in colr:
                    for (hf, lc, ln_, gc) in xpieces(c0, n):
                        stt(XH[hf][:, oc_, lc:lc + ln_], PS[pb][:, gc:gc + ln_], TAB[:, row, 5, oc_:oc_ + 1],
                            XH[hf][:, oc_, lc:lc + ln_], ALU.mult, ALU.add,
                            ["ps%d" % pb, "TAB", "xh%d_%d" % (hf, oc_)], ["xh%d_%d" % (hf, oc_)])

            ydst = ysT if sample else yT
            ycol0 = 0 if sample else slot * 512
            ydst_v = ydst.rearrange("(dc p) t -> p dc t", p=128)
            fo = [0]

            def fin_out(hf, lc, ln_, gc):
                def f(dc):
                    fi = fo[0] % 2
                    return (STF[fi][:, 0:ln_], "stf%d" % fi)
                return f
            for (c0, n, row) in colr:
                for (hf, lc, ln_, gc) in xpieces(c0, n):
                    for dc in range(DC):
                        act(sq[:, dc, 0:ln_], XH[hf][:, dc, lc:lc + ln_], AF.Square, ["xh%d_%d" % (hf, dc)],
                            ["sq%d" % dc])
                    for dc in range(DC):
                        mm(PS[7][:, 0:ln_], ones_bf[:, :], sq[:, dc, 0:ln_], dc == 0, dc == DC - 1,
                           ["sq%d" % dc, "ones"], ["ps7"])
                    rstd_from_ps(PS[7][:, 0:ln_], ln_, 1.0 / D, "ps7")
                    for dc in range(DC):
                        t = tmprr[0] % 4
                        tmprr[0] += 1
                        tt(TMP[t][:, 0:ln_], XH[hf][:, dc, lc:lc + ln_], rstd[:, 0:ln_], ALU.mult,
                           ["xh%d_%d" % (hf, dc), "rstd"], ["tmp%d" % t])
                        fi = fo[0] % 2
                        fo[0] += 1
                        act(STF[fi][:, 0:ln_], TMP[t][:, 0:ln_], AF.Identity, ["tmp%d" % t, "TAB"], ["stf%d" % fi],
                            bias=TAB[:, row, 7, dc:dc + 1], scale=TAB[:, row, 6, dc:dc + 1])
                        out_dma(ydst_v[:, dc, ycol0 + gc:ycol0 + gc + ln_], STF[fi][:, 0:ln_], ["stf%d" % fi])

        P.add_writes(P.ops["pe"][-1], ["kbuf0", "kbuf1", "vbuf0", "vbuf1"])
        for slot in range(NSLOT if STOP >= 3 else 0):
            do_slot(slot, False)
        if STOP >= 4:
            do_slot(0, True)

        P.op("sp", lambda e: e.nop(), [], [], None)
        fin = P.ops["sp"][-1]
        fin.deps = [(o, True) for o in P.out_ops]

        P.finalize()
        if _os.environ.get('SEMDBG'):
            print('SEMDBG', {e: P.eng_sem[e].total for e in P.eng_sem}, 'ninstr', {e: len(P.ops[e]) for e in P.ops})
        with nc.Block() as block:
            @block.sync
            def _(e):
                P.emit("sp", e)

            @block.gpsimd
            def _(e):
                P.emit("pool", e)

            @block.tensor
            def _(e):
                P.emit("pe", e)

            @block.scalar
            def _(e):
                P.emit("act", e)

            @block.vector
            def _(e):
                P.emit("dve", e)
    return nc


def _t5_bucket(rel):
    nb = 16
    max_exact = 8
    ret = np.where(rel > 0, nb, 0)
    n = np.abs(rel)
    nf = np.maximum(n, 1).astype(np.float32)
    large = max_exact + (np.log(nf / max_exact) / math.log(128 / max_exact) * (nb - max_exact)).astype(np.int32)
    large = np.minimum(large, nb - 1)
    return ret + np.where(n < max_exact, n, large)


def _consts():
    d = np.arange(NG) - 639
    bk = _t5_bucket(d.astype(np.int64))
    oh = np.zeros((32, NG), np.float32)
    oh[bk, np.arange(NG)] = 1.0
    oh[15, :] -= 1.0
    p = np.arange(128)[:, None]
    m = np.arange(1024)[None, :] - 384
    cm = np.where((p // 64) <= np.floor_divide(m, 64), 0.0, NEG).astype(np.float32)
    return oh, cm


_NC_CACHE = {}


def kernel(x_prompt, x_sample, cache_k, cache_v, c_prompt, c_sample, rel_bias,
           w_ada, b_ada, w_ada_final, b_ada_final, g_mix, g_ffn, g_final,
           w_in, mlp_ln_g, mlp_ln_b, w_s, b_s, lambda_q1, lambda_k1, lambda_q2, lambda_k2,
           sub_g, w_out, w_ffn_in, w_ffn_out):
    f = np.float32
    A = lambda a: np.ascontiguousarray(np.asarray(a, dtype=f))
    x_prompt = A(x_prompt); x_sample = A(x_sample); cache_k = A(cache_k); cache_v = A(cache_v)
    B, S, _ = x_prompt.shape
    NT = S // 512
    NSLOT = NT // 4
    assert NT % 4 == 0 and B == 2
    if NT not in _NC_CACHE:
        _NC_CACHE[NT] = build_nc(NT)
    nc = _NC_CACHE[NT]
    oh, cm = _consts()

    def fm(v, n):
        return A(np.asarray(v, f).reshape(n, 128).T)

    shared = {
        "w_ada": A(w_ada)[0], "b_adaT": fm(np.asarray(b_ada)[0], 96),
        "w_adaf": A(w_ada_final), "b_adafT": fm(b_ada_final, 32),
        "g3T": A(np.stack([fm(np.asarray(g_mix)[0], 16), fm(np.asarray(g_ffn)[0], 16), fm(g_final, 16)], axis=1)),
        "w_in": A(w_in)[0],
        "lnGB": A(np.broadcast_to(np.stack([np.asarray(mlp_ln_g, f)[0], np.asarray(mlp_ln_b, f)[0]])[None], (128, 2, 1024))),
        "w_sT": A(np.transpose(np.asarray(w_s, f)[0], (2, 0, 1))),
        "bsB": A(np.broadcast_to(np.asarray(b_s, f)[0][None], (128, 8, 128))),
        "lamv": A(np.broadcast_to(np.concatenate([np.asarray(v, f)[0] for v in
                                                  (lambda_q1, lambda_k1, lambda_q2, lambda_k2)])[None], (128, 256))),
        "subgT": A(np.asarray(sub_g, f)[0].reshape(128, 1)),
        "relb": A(rel_bias), "ohp": oh, "cmask": cm,
        "w_out": A(w_out)[0], "w_ffn_in": A(w_ffn_in)[0], "w_ffn_out": A(w_ffn_out)[0],
    }
    in_maps = []
    for c in range(8):
        b, j = c // 4, c % 4
        pad = (3 - j) * 512
        xs = np.zeros((NT * 512, D), f)
        nreal = NT * 512 - pad
        xs[pad:] = x_prompt[b, :nreal]
        pf = np.zeros((128, 4), f)
        for r in range(3):
            if r < 3 - j:
                pf[:, r] = NEG
        s0 = 2 * c
        xsm = np.concatenate([x_sample[s0], x_sample[s0 + 1]], axis=0)
        ck = np.stack([np.transpose(cache_k[0, s0 + st], (1, 2, 0)) for st in range(2)])
        cv = np.stack([np.transpose(cache_v[0, s0 + st].reshape(16, 128, NH, 128), (2, 1, 0, 3)) for st in range(2)])
        crow = np.stack([np.asarray(c_prompt, f)[b], np.asarray(c_sample, f)[s0], np.asarray(c_sample, f)[s0 + 1]])
        cTa = np.transpose(crow.reshape(3, DC, 128), (2, 1, 0))
        m = dict(shared)
        m.update({"xsT": A(xs.T), "xsmpT": A(xsm.T), "ckT": A(ck), "cvB": A(cv), "cT": A(cTa), "padflag": pf})
        in_maps.append(m)

    if KCORES < 8:
        res = run_bass_kernel_spmd(nc, in_maps[:KCORES], core_ids=list(range(KCORES)))
        R = list(res.results)
        R = R + [{k: np.zeros_like(v) for k, v in R[0].items()} for _ in range(8 - KCORES)]
    else:
        res = run_bass_kernel_spmd(nc, in_maps, core_ids=list(range(8)))
        R = res.results
    DB, T = x_sample.shape[0], x_sample.shape[1]
    y_p = np.zeros((B, S, D), f)
    y_s = np.zeros((DB, T, D), f)
    nk_p = np.zeros((1, B, S, NH, 128), f)
    nv_p = np.zeros((1, B, S, NH, 128), f)
    nk_s = np.zeros((1, DB, T, NH, 128), f)
    nv_s = np.zeros((1, DB, T, NH, 128), f)
    ngv_s = np.zeros((1, DB, T, 8, 128), f)
    for c in range(8):
        b, j = c // 4, c % 4
        r = R[c]
        for i in range(NSLOT):
            p0 = (4 * i + j) * 512
            y_p[b, p0:p0 + 512] = r["yT"][:, i * 512:(i + 1) * 512].T
            nk_p[0, b, p0:p0 + 512] = np.transpose(r["kT_out"][:, :, i * 512:(i + 1) * 512], (2, 0, 1))
            nv_p[0, b, p0:p0 + 512] = r["v_out"][i * 512:(i + 1) * 512].reshape(512, NH, 128)
        for st in range(2):
            s = 2 * c + st
            y_s[s] = r["ysT"][:, st * 64:(st + 1) * 64].T
            nk_s[0, s] = np.transpose(r["ksT_out"][:, :, st * 64:(st + 1) * 64], (2, 0, 1))
            nv_s[0, s] = r["vs_out"][st * 64:(st + 1) * 64].reshape(64, NH, 128)
            ngv_s[0, s] = r["gvs_out"][st * 64:(st + 1) * 64].reshape(64, 8, 128)
    return (y_p, y_s, nk_p, nv_p, nk_s, nv_s, ngv_s)
```
